# Optimizing a Trainium2 kernel written in Bass

```python
import jax, jax.numpy as jnp
from jax import lax
import numpy as np

D_MODEL = 1024
BATCH = 8
SEQ = 2048
DEPTH = 1
DEC_BATCH = 128
DEC_SEQ = 4
PAST_LEN = 8192
PAGE_SIZE = 128

HEAD_DIM = 64
N_HEADS_A = 8
N_HEADS_B = 8
WIDTH_A = N_HEADS_A * HEAD_DIM
WIDTH_B = N_HEADS_B * HEAD_DIM
MIX_WIDTH = WIDTH_A + WIDTH_B
IN_WIDTH = 2 * WIDTH_A + 3 * WIDTH_B
CHUNK = 128
DILATED_PATTERNS = ((128, 1), (512, 4), (2048, 16))
MAX_WINDOW = max(w for w, _ in DILATED_PATTERNS)
ATTN_BLOCK = 128
D_FF = ((8 * D_MODEL + 3 * 256 - 1) // (3 * 256)) * 256
EPS = 1e-6

kernel_name = "hymba_gmlp_dilated_swa_decode"

F32 = jnp.float32


def rms_norm(x, g):
    xf = x.astype(F32)
    y = xf * lax.rsqrt(jnp.mean(xf * xf, axis=-1, keepdims=True) + EPS)
    return (y * g.astype(F32)).astype(x.dtype)


def layer_norm(x, g, b):
    xf = x.astype(F32)
    mu = jnp.mean(xf, axis=-1, keepdims=True)
    xc = xf - mu
    y = xc * lax.rsqrt(jnp.mean(xc * xc, axis=-1, keepdims=True) + EPS)
    return y * g.astype(F32) + b.astype(F32)


def _in_proj(x, g_attn, w_in):
    z = rms_norm(x, g_attn) @ w_in
    o1 = WIDTH_A
    o2 = o1 + WIDTH_A
    o3 = o2 + WIDTH_B
    o4 = o3 + WIDTH_B
    return z[..., :o1], z[..., o1:o2], z[..., o2:o3], z[..., o3:o4], z[..., o4:]


def _gmlp_features(zu, zv, ln_g, ln_b):
    u = jax.nn.gelu(zu.astype(F32))
    vn = layer_norm(jax.nn.gelu(zv.astype(F32)), ln_g, ln_b)
    return u, vn


def _spatial_gate(u, vn, w_s, b_s):
    N, L, _ = vn.shape
    c = min(CHUNK, L)
    nc = L // c
    tri = jnp.tril(jnp.ones((c, c), dtype=bool))
    ws = jnp.where(tri, w_s[:, :c, :c].astype(F32), 0.0)
    vr = vn.reshape(N, nc, c, N_HEADS_A, HEAD_DIM)
    s = jnp.einsum('hij,bnjhd->bnihd', ws, vr)
    s = s + b_s[:, :c].astype(F32).T[None, None, :, :, None]
    return u * s.reshape(N, L, WIDTH_A)


def _qkv(zq, zk, zv, g_q, g_k):
    shp = zq.shape[:-1] + (N_HEADS_B, HEAD_DIM)
    q = rms_norm(zq.reshape(shp), g_q).astype(F32) * (HEAD_DIM ** -0.5)
    k = rms_norm(zk.reshape(shp), g_k).astype(F32)
    v = zv.reshape(shp).astype(F32)
    return q, k, v


def _band_stats(q, k, v, n_back):
    N, L, H, Dh = q.shape
    blk = min(ATTN_BLOCK, L)
    nb = -(-L // blk)
    Lp = nb * blk
    pad = ((0, 0), (0, Lp - L), (0, 0), (0, 0))
    q, k, v = jnp.pad(q, pad), jnp.pad(k, pad), jnp.pad(v, pad)
    qb = q.reshape(N, nb, blk, H, Dh)

    def with_prev(a):
        ab = a.reshape(N, nb, blk, H, Dh)
        prev = jnp.pad(ab, ((0, 0), (1, 0), (0, 0), (0, 0), (0, 0)))[:, :-1]
        return jnp.concatenate([prev, ab], axis=2)

    kk, vv = with_prev(k), with_prev(v)
    s = jnp.einsum('nbihd,nbjhd->nbhij', qb, kk)
    qpos = jnp.arange(nb)[:, None] * blk + jnp.arange(blk)[None, :]
    kpos = jnp.arange(nb)[:, None] * blk - blk + jnp.arange(2 * blk)[None, :]
    dist = qpos[:, :, None] - kpos[:, None, :]
    valid = (dist >= 0) & (dist <= n_back) & (kpos[:, None, :] >= 0)
    s = jnp.where(valid[None, :, None], s, -jnp.inf)
    m = jnp.max(s, axis=-1)
    p = jnp.exp(s - m[..., None])
    l = jnp.sum(p, axis=-1)
    acc = jnp.einsum('nbhij,nbjhd->nbihd', p, vv).reshape(N, Lp, H, Dh)[:, :L]
    m = jnp.swapaxes(m, 2, 3).reshape(N, Lp, H)[:, :L]
    l = jnp.swapaxes(l, 2, 3).reshape(N, Lp, H)[:, :L]
    return m, l, acc


def _dilated_branch_prompt(q, k, v, window, dil):
    B, S, H, Dh = q.shape
    L = S // dil

    def to_res(a):
        return jnp.swapaxes(a.reshape(B, L, dil, H, Dh), 1, 2).reshape(B * dil, L, H, Dh)

    def from_res(a):
        a = a.reshape((B, dil) + a.shape[1:])
        return jnp.swapaxes(a, 1, 2).reshape((B, S) + a.shape[3:])

    m, l, acc = _band_stats(to_res(q), to_res(k), to_res(v), window // dil)
    return from_res(m), from_res(l), from_res(acc)


def _dilated_branch_sample(q, k_all, v_all, window, dil):
    T = q.shape[1]
    Lb = k_all.shape[1] - T
    n_keys = window // dil + 1
    idx = Lb + jnp.arange(T)[:, None] - dil * jnp.arange(n_keys)[None, :]
    valid = (idx >= 0) & (idx + (PAST_LEN - Lb) >= 0)
    idx_c = jnp.clip(idx, 0)
    kg = k_all[:, idx_c]
    vg = v_all[:, idx_c]
    s = jnp.einsum('nthd,ntjhd->nthj', q, kg)
    s = jnp.where(valid[None, :, None, :], s, -jnp.inf)
    m = jnp.max(s, axis=-1)
    p = jnp.exp(s - m[..., None])
    l = jnp.sum(p, axis=-1)
    acc = jnp.einsum('nthj,ntjhd->nthd', p, vg)
    return m, l, acc


def _combine(stats):
    m_all = jnp.stack([st[0] for st in stats])
    l_all = jnp.stack([st[1] for st in stats])
    acc_all = jnp.stack([st[2] for st in stats])
    w = jnp.exp(m_all - jnp.max(m_all, axis=0, keepdims=True))
    num = jnp.sum(w[..., None] * acc_all, axis=0)
    den = jnp.sum(w * l_all, axis=0)
    return num / den[..., None]


def _finish(x, a_out, b_out, g_out_a, g_out_b, w_o, g_ffn, w_gate, w_up, w_down):
    lead = b_out.shape[:2]
    mix = jnp.concatenate([rms_norm(a_out, g_out_a), rms_norm(b_out.reshape(lead + (WIDTH_B,)), g_out_b)], axis=-1)
    x = x + mix.astype(x.dtype) @ w_o
    h = rms_norm(x, g_ffn)
    return x + (jax.nn.silu(h @ w_gate) * (h @ w_up)) @ w_down


def setup_inputs(seed: int = 0) -> dict:
    key = jax.random.key(seed)
    ks = jax.random.split(key, 24)
    win_buf = min(MAX_WINDOW, PAST_LEN)
    nrm = lambda k, shp, s: jax.random.normal(k, shp, F32) * s
    gain = lambda k, shp: 1.0 + 0.05 * jax.random.normal(k, shp, F32)
    return {
        "x_prompt": nrm(ks[0], (BATCH, SEQ, D_MODEL), 1.0),
        "x_sample": nrm(ks[1], (DEC_BATCH, DEC_SEQ, D_MODEL), 1.0),
        "cache_k": nrm(ks[2], (DEPTH, DEC_BATCH, win_buf, N_HEADS_B, HEAD_DIM), 1.0),
        "cache_v": nrm(ks[3], (DEPTH, DEC_BATCH, win_buf, N_HEADS_B, HEAD_DIM), 1.0),
        "g_attn": gain(ks[4], (DEPTH, D_MODEL)),
        "w_in": nrm(ks[5], (DEPTH, D_MODEL, IN_WIDTH), D_MODEL ** -0.5),
        "ln_v_g": gain(ks[6], (DEPTH, WIDTH_A)),
        "ln_v_b": nrm(ks[7], (DEPTH, WIDTH_A), 0.02),
        "w_s": nrm(ks[8], (DEPTH, N_HEADS_A, CHUNK, CHUNK), CHUNK ** -0.5),
        "b_s": 1.0 + nrm(ks[9], (DEPTH, N_HEADS_A, CHUNK), 0.02),
        "g_q": gain(ks[10], (DEPTH, HEAD_DIM)),
        "g_k": gain(ks[11], (DEPTH, HEAD_DIM)),
        "g_out_a": gain(ks[12], (DEPTH, WIDTH_A)),
        "g_out_b": gain(ks[13], (DEPTH, WIDTH_B)),
        "w_o": nrm(ks[14], (DEPTH, MIX_WIDTH, D_MODEL), MIX_WIDTH ** -0.5),
        "g_ffn": gain(ks[15], (DEPTH, D_MODEL)),
        "w_gate": nrm(ks[16], (DEPTH, D_MODEL, D_FF), D_MODEL ** -0.5),
        "w_up": nrm(ks[17], (DEPTH, D_MODEL, D_FF), D_MODEL ** -0.5),
        "w_down": nrm(ks[18], (DEPTH, D_FF, D_MODEL), D_FF ** -0.5),
    }


def reference(x_prompt, x_sample, cache_k, cache_v, g_attn, w_in, ln_v_g, ln_v_b, w_s, b_s,
              g_q, g_k, g_out_a, g_out_b, w_o, g_ffn, w_gate, w_up, w_down):
    xp, xs = x_prompt, x_sample
    win_p = min(MAX_WINDOW, xp.shape[1])
    kp_l, vp_l, ksm_l, vsm_l, cs_l = [], [], [], [], []
    for l in range(DEPTH):
        zu, zv, zq, zk, zvb = _in_proj(xp, g_attn[l], w_in[l])
        u, vn = _gmlp_features(zu, zv, ln_v_g[l], ln_v_b[l])
        a_out = _spatial_gate(u, vn, w_s[l], b_s[l])
        q, k, v = _qkv(zq, zk, zvb, g_q[l], g_k[l])
        b_out = _combine([_dilated_branch_prompt(q, k, v, wd, dl) for wd, dl in DILATED_PATTERNS])
        new_xp = _finish(xp, a_out, b_out, g_out_a[l], g_out_b[l], w_o[l], g_ffn[l], w_gate[l], w_up[l], w_down[l])
        kp_l.append(k[:, -win_p:].astype(xp.dtype))
        vp_l.append(v[:, -win_p:].astype(xp.dtype))

        zu, zv, zq, zk, zvb = _in_proj(xs, g_attn[l], w_in[l])
        u, vn = _gmlp_features(zu, zv, ln_v_g[l], ln_v_b[l])
        a_out = _spatial_gate(u, vn, w_s[l], b_s[l])
        q, k, v = _qkv(zq, zk, zvb, g_q[l], g_k[l])
        k_all = jnp.concatenate([cache_k[l].astype(F32), k], axis=1)
        v_all = jnp.concatenate([cache_v[l].astype(F32), v], axis=1)
        b_out = _combine([_dilated_branch_sample(q, k_all, v_all, wd, dl) for wd, dl in DILATED_PATTERNS])
        new_xs = _finish(xs, a_out, b_out, g_out_a[l], g_out_b[l], w_o[l], g_ffn[l], w_gate[l], w_up[l], w_down[l])
        ksm_l.append(k.astype(xs.dtype))
        vsm_l.append(v.astype(xs.dtype))
        cs_l.append(vn.astype(xs.dtype))
        xp, xs = new_xp, new_xs
    return (xp, xs, jnp.stack(kp_l), jnp.stack(vp_l), jnp.stack(ksm_l), jnp.stack(vsm_l), jnp.stack(cs_l))
```

```python
import numpy as np
import concourse.bass as bass
import concourse.mybir as mybir
from concourse.bass_utils import run_bass_kernel_spmd

F32 = mybir.dt.float32
BF16 = mybir.dt.bfloat16
I32 = mybir.dt.int32
AF = mybir.ActivationFunctionType
ALU = mybir.AluOpType
AX = mybir.AxisListType

ENGS = ("sync", "scalar", "gpsimd", "vector", "tensor")
D = 1024
S = 2048
NT = 16
DFF = 2816
NF = 22
EPS = 1e-6
NEG = -30000.0


class Prog:
    def __init__(self, nc, n_dma_sems=40):
        self.nc = nc
        self.ops = {e: [] for e in ENGS}
        self.cnt = {e: 0 for e in ENGS}
        self.sem = {e: nc.alloc_semaphore("prog_" + e) for e in ENGS}
        self.seen = {e: {} for e in ENGS}
        self.dma_sems = [nc.alloc_semaphore("dmas%d" % i) for i in range(n_dma_sems)]
        self.dma_cnt = [0] * n_dma_sems
        self.dma_rr = 0
        self.last_w = {}
        self.readers = {}
        self.n_inst = 0
        self.pending = {e: False for e in ENGS}

    def _wait(self, eng, ev):
        if ev[0] == "e":
            if ev[1] == eng and eng == "tensor":
                return
            key, sem, val = ev[1], self.sem[ev[1]], ev[2]
        else:
            key, sem, val = ("d", ev[1]), self.dma_sems[ev[1]], ev[2]
        if self.seen[eng].get(key, 0) >= val:
            return
        self.seen[eng][key] = val
        self.ops[eng].append(lambda e, sem=sem, val=val: e.wait_ge(sem, val))
        self.n_inst += 1

    def _deps(self, eng, reads, writes, waits):
        for ev in waits:
            self._wait(eng, ev)
        for k in reads:
            ev = self.last_w.get(k)
            if ev is not None:
                self._wait(eng, ev)
        for k in writes:
            ev = self.last_w.get(k)
            if ev is not None:
                self._wait(eng, ev)
            for ev in self.readers.get(k, ()):
                self._wait(eng, ev)

    def _commit(self, ev, reads, writes):
        for k in reads:
            self.readers.setdefault(k, []).append(ev)
        for k in writes:
            self.last_w[k] = ev
            self.readers[k] = []

    def op(self, eng, fn, reads=(), writes=(), waits=(), signal=True):
        self._deps(eng, reads, writes, waits)
        if signal:
            self.cnt[eng] += 1
            ev = ("e", eng, self.cnt[eng])
            sem = self.sem[eng]
            self.ops[eng].append(lambda e, fn=fn, sem=sem: fn(e).then_inc(sem, 1))
            self.pending[eng] = False
        else:
            ev = ("e", eng, self.cnt[eng] + 1)
            self.ops[eng].append(lambda e, fn=fn: fn(e))
            self.pending[eng] = True
        self.n_inst += 1
        self._commit(ev, reads, writes)
        return ev

    def dma(self, eng, out, in_, reads=(), writes=(), waits=(), **kw):
        self._deps(eng, reads, writes, waits)
        i = self.dma_rr
        self.dma_rr = (self.dma_rr + 1) % len(self.dma_sems)
        if self.dma_cnt[i] > 0:
            self._wait(eng, ("d", i, self.dma_cnt[i]))
        self.dma_cnt[i] += 16
        ev = ("d", i, self.dma_cnt[i])
        sem = self.dma_sems[i]
        self.ops[eng].append(
            lambda e, out=out, in_=in_, sem=sem, kw=kw: e.dma_start(out=out, in_=in_, **kw).then_inc(sem, 16))
        self.n_inst += 1
        self._commit(ev, reads, writes)
        return ev

    def barrier(self, skip=()):
        skipset = set(skip)
        assert not any(self.pending.values()), "non-signalled op pending at barrier"
        for e in ENGS:
            for o in ENGS:
                if o != e and self.cnt[o] > 0:
                    self._wait(e, ("e", o, self.cnt[o]))
            for i, c in enumerate(self.dma_cnt):
                while c > 0 and ("d", i, c) in skipset:
                    c -= 16
                if c > 0:
                    self._wait(e, ("d", i, c))

    def finish(self):
        for i, c in enumerate(self.dma_cnt):
            if c > 0:
                self._wait("sync", ("d", i, c))
        for o in ENGS:
            if o != "sync" and self.cnt[o] > 0:
                self._wait("sync", ("e", o, self.cnt[o]))

    def emit(self):
        with self.nc.Block() as block:
            for name in ENGS:
                ops = self.ops[name]

                def body(e, ops=ops):
                    for f in ops:
                        f(e)
                getattr(block, name)(body)


class Arena:
    LO = 16512
    HI = 229344

    def __init__(self, nc):
        self.nc = nc
        self.items = []
        self.t = {}

    def decl(self, name, shape, dtype, p0, p1):
        esz = 4 if dtype in (F32, I32) else 2
        n = 1
        for s in shape[1:]:
            n *= s
        nbytes = (n * esz + 31) // 32 * 32
        self.items.append((name, list(shape), dtype, nbytes, p0, p1))

    def build(self):
        placed = []
        order = sorted(self.items, key=lambda it: -it[3])
        for name, shape, dtype, nbytes, p0, p1 in order:
            cands = [self.LO] + sorted(e for (_, e, _, _) in placed)
            off = None
            for c in cands:
                ok = True
                for (o2, e2, q0, q1) in placed:
                    if not (p1 < q0 or q1 < p0) and not (c + nbytes <= o2 or e2 <= c):
                        ok = False
                        break
                if ok and c + nbytes <= self.HI:
                    off = c
                    break
            if off is None:
                raise RuntimeError("SBUF plan failed for %s (%d bytes, phases %d-%d)" % (name, nbytes, p0, p1))
            placed.append((off, off + nbytes, p0, p1))
            self.t[name] = self.nc.alloc_sbuf_tensor_at(name, shape, dtype, offset=off)
        return self.t


DEBUG = False


def build_program():
    nc = bass.Bass("TRN2", target_bir_lowering=False)
    P = Prog(nc)

    def din(name, shape, dt=F32):
        return nc.dram_tensor(name, list(shape), dt, kind="ExternalInput").ap()

    def dout(name, shape, dt=F32):
        return nc.dram_tensor(name, list(shape), dt, kind="ExternalOutput").ap()

    xp = din("xp", [S, D])
    xs = din("xs", [64, D])
    ck = din("ck", [16, 2048, 512])
    cv = din("cv", [16, 2048, 512])
    g_attn = din("g_attn", [D])
    w_in = din("w_in", [D, 2560])
    ln_v_g = din("ln_v_g", [512])
    ln_v_b = din("ln_v_b", [512])
    w_sT = din("w_sT", [8, 128, 128])
    b_s = din("b_s", [8, 128])
    g_q = din("g_q", [64])
    g_k = din("g_k", [64])
    g_out_a = din("g_out_a", [512])
    g_out_b = din("g_out_b", [512])
    w_o = din("w_o", [D, D])
    g_ffn = din("g_ffn", [D])
    w_gate = din("w_gate", [D, DFF])
    w_up = din("w_up", [D, DFF])
    w_down = din("w_down", [DFF, D])

    y_p = dout("y_p", [S, D])
    y_s = dout("y_s", [64, D])
    nk_p = dout("nk_p", [S, 512])
    nv_p = dout("nv_p", [S, 512])
    nk_s = dout("nk_s", [64, 512])
    nv_s = dout("nv_s", [64, 512])
    nvc_s = dout("nvc_s", [64, 512])
    qs_scr = dout("qs_scr", [64, 512], BF16)
    tscr = dout("tscr", [64])
    if DEBUG:
        d_mixTa = dout("d_mixTa", [128, 4, S], BF16)
        d_mixTb = dout("d_mixTb", [128, 4, S], BF16)
        d_mixTs = dout("d_mixTs", [128, 8, 64], BF16)
        d_u = dout("d_u", [128, 512])
        d_a32 = dout("d_a32", [128, 512])
        d_s = dout("d_s", [128, 512])
        d_vnbf = dout("d_vnbf", [128, 512], BF16)
        d_WsT = dout("d_WsT", [128, 8, 128], BF16)
        d_bsl = dout("d_bsl", [40, 128], BF16)
        d_blockind = dout("d_blockind", [40, 512], BF16)

    A = Arena(nc)
    A.decl("ident", [128, 128], BF16, 0, 5)
    A.decl("ones", [128, 128], BF16, 0, 2)
    A.decl("g_ffn_bc", [128, D], F32, 0, 5)
    A.decl("gob_bc", [64, 512], F32, 0, 5)
    A.decl("gob_col", [128, 4], F32, 0, 2)
    A.decl("st", [128, 17 * 48], F32, 0, 1)
    A.decl("st2", [128, 64], F32, 0, 7)
    A.decl("selw", [128, 127], BF16, 0, 7)
    A.decl("mb", [128, 4 * 24], F32, 0, 7)
    A.decl("ks32", [64, 512], F32, 1, 1)
    A.decl("vs32", [64, 512], F32, 1, 1)
    A.decl("qs_bf", [64, 512], BF16, 1, 1)
    A.decl("mixTs", [128, 8, 64], BF16, 0, 7)
    A.decl("g_attn_bc", [128, D], F32, 0, 1)
    A.decl("ln_g_bc", [128, 512], F32, 0, 1)
    A.decl("ln_b_bc", [128, 512], F32, 0, 1)
    A.decl("goa_bc", [128, 512], F32, 0, 1)
    A.decl("gq8_bc", [128, 64], F32, 0, 1)
    A.decl("gk_bc", [128, 64], F32, 0, 1)
    A.decl("WsT", [128, 8, 128], BF16, 0, 1)
    A.decl("WsTs", [64, 8, 64], BF16, 0, 1)
    A.decl("bs32", [8, 128], F32, 0, 1)
    A.decl("bss32", [8, 64], F32, 0, 1)
    A.decl("bstmp", [8, 128], F32, 0, 1)
    A.decl("bsl", [40, 128], BF16, 0, 1)
    A.decl("bsls", [40, 64], BF16, 0, 1)
    A.decl("blockind", [40, 512], BF16, 0, 1)
    A.decl("Win", [128, 8, 2560], BF16, 0, 1)
    A.decl("wst0", [128, 2560], F32, 0, 0)
    A.decl("wst1", [128, 2560], F32, 0, 0)
    A.decl("wst2", [128, 2560], F32, 0, 0)
    A.decl("fst0", [128, 8, 512], F32, 3, 4)
    A.decl("fst1", [128, 8, 512], F32, 3, 4)
    A.decl("hTs", [128, 8, 64], BF16, 0, 1)
    for nm in ("xt0", "xt1"):
        A.decl(nm, [128, D], F32, 1, 1)
    A.decl("h_bf", [128, D], BF16, 1, 1)
    A.decl("h_bf_1", [128, D], BF16, 1, 1)
    A.decl("u_1", [128, 512], F32, 1, 1)
    A.decl("vn_bf_1", [128, 512], BF16, 1, 1)
    A.decl("q_bf_1", [128, 512], BF16, 1, 1)
    A.decl("k_bf_1", [128, 512], BF16, 1, 1)
    A.decl("a_bf_1", [128, 512], BF16, 1, 1)
    A.decl("zk_sb", [128, 512], F32, 1, 1)
    A.decl("u", [128, 512], F32, 1, 1)
    A.decl("gv", [128, 512], F32, 1, 1)
    A.decl("wtmp", [128, 512], F32, 1, 1)
    A.decl("vn32", [128, 512], F32, 1, 1)
    A.decl("vn_bf", [128, 512], BF16, 1, 1)
    A.decl("sq", [128, D], F32, 1, 1)
    A.decl("k32_0", [128, 512], F32, 1, 1)
    A.decl("k32_1", [128, 512], F32, 1, 1)
    A.decl("v32_0", [128, 512], F32, 1, 1)
    A.decl("v32_1", [128, 512], F32, 1, 1)
    A.decl("tq", [128, 512], F32, 1, 1)
    A.decl("q_bf", [128, 512], BF16, 1, 1)
    A.decl("k_bf", [128, 512], BF16, 1, 1)
    A.decl("a32", [128, 512], F32, 1, 1)
    A.decl("a_bf", [128, 512], BF16, 1, 1)
    A.decl("hT", [128, 8, S], BF16, 1, 2)
    A.decl("qT", [128, 4, S], BF16, 1, 2)
    A.decl("kT", [128, 4, S], BF16, 1, 2)
    A.decl("Wv", [128, 8, 512], BF16, 2, 2)
    A.decl("mask2", [128, 256], BF16, 0, 2)
    A.decl("mixTa", [128, 4, S], BF16, 1, 3)
    A.decl("mixTb", [128, 4, S], BF16, 2, 3)
    A.decl("accA", [128, S], F32, 2, 2)
    A.decl("accB", [128, S], F32, 2, 2)
    A.decl("bT", [128, 4, S], F32, 2, 2)
    A.decl("ssqb", [128, S], F32, 2, 2)
    A.decl("tmpd", [128, S], F32, 2, 2)
    A.decl("sqb", [128, S], BF16, 2, 2)
    for i in range(3):
        A.decl("Vaug%d" % i, [128, 192], BF16, 2, 2)
    for i in range(4):
        A.decl("PT%d" % i, [128, 256], BF16, 2, 2)
    A.decl("Wo", [128, 8, D], BF16, 2, 4)
    A.decl("Wg", [128, 8, DFF], BF16, 3, 7)
    A.decl("Wu", [128, 8, DFF], BF16, 3, 7)
    A.decl("Wd", [128, NF, D], BF16, 5, 7)
    for i in range(4):
        A.decl("xa%d" % i, [128, D], F32, 3, 4)
    A.decl("xio0", [128, D], F32, 5, 7)
    A.decl("xio1", [128, D], F32, 5, 7)
    A.decl("h2_bf", [128, D], BF16, 5, 7)
    A.decl("Wo7", [128, 8, D], BF16, 6, 7)
    A.decl("junk3", [128, D], BF16, 3, 4)
    A.decl("h2Ts", [128, 8, 64], BF16, 7, 7)
    A.decl("actTs", [128, NF, 64], BF16, 7, 7)
    A.decl("h2T", [128, 8, 512], BF16, 5, 5)
    A.decl("actT", [128, NF, 512], BF16, 5, 5)
    A.decl("sil0", [128, 512], BF16, 5, 7)
    A.decl("sil1", [128, 512], BF16, 5, 7)
    for i in range(2):
        A.decl("Kt%d" % i, [128, 3, 512], BF16, 5, 5)
        A.decl("Vt%d" % i, [128, 3, 512], BF16, 5, 5)
        A.decl("qb%d" % i, [128, 512], BF16, 5, 5)
        A.decl("wV%d" % i, [128, 3, 512], BF16, 5, 5)
    for i in range(4):
        A.decl("pbf%d" % i, [128, 24], BF16, 5, 5)
    A.decl("s24_1", [128, 24], F32, 5, 5)
    A.decl("prod", [128, 512], F32, 5, 6)
    A.decl("s24", [128, 24], F32, 5, 5)
    A.decl("ksh", [64, 4, 512], F32, 6, 6)
    A.decl("vsh", [64, 4, 512], F32, 6, 6)
    A.decl("qs6", [64, 512], BF16, 6, 6)
    A.decl("trow_i", [1, 64], I32, 6, 6)
    A.decl("trow_f", [1, 64], F32, 6, 6)
    A.decl("sprod", [64, 512], F32, 6, 6)
    A.decl("snum", [64, 512], F32, 6, 6)
    A.decl("sb_bf", [64, 512], BF16, 6, 6)
    T = A.build()

    bank = [nc.alloc_psum_tensor("bank%d" % i, [128, 512], F32) for i in range(8)]

    def bankbf(i):
        return bank[i][:].bitcast(BF16)

    st = T["st"]

    def MM(out, lhsT, rhs, start, stop, reads, writes, signal=None):
        if signal is None:
            signal = bool(stop)
        return P.op("tensor", lambda e: e.matmul(out, lhsT=lhsT, rhs=rhs, start=start, stop=stop,
                                                 skip_group_check=True), reads=reads, writes=writes, signal=signal)

    def TR(out, in_, ident, reads, writes, signal=True):
        return P.op("tensor", lambda e: e.transpose(out=out, in_=in_, identity=ident), reads=reads, writes=writes,
                    signal=signal)

    def ACT(out, in_, func, reads, writes, **kw):
        return P.op("scalar", lambda e: e.activation(out=out, in_=in_, func=func, **kw), reads=reads, writes=writes)

    def ACOPY(out, in_, reads, writes):
        return P.op("scalar", lambda e: e.copy(out=out, in_=in_), reads=reads, writes=writes)

    def V(fn, reads, writes):
        return P.op("vector", fn, reads=reads, writes=writes)

    def G(fn, reads, writes):
        return P.op("gpsimd", fn, reads=reads, writes=writes)

    def rstd_chain(ap, scale, key):
        V(lambda e: e.tensor_scalar(out=ap, in0=ap, scalar1=scale, scalar2=EPS, op0=ALU.mult, op1=ALU.add), [key], [key])
        ACT(ap, ap, AF.Sqrt, [key], [key])
        V(lambda e: e.reciprocal(out=ap, in_=ap), [key], [key])

    ident, ones = T["ident"], T["ones"]
    G(lambda e: e.memset(ident[:], 1.0), [], ["ident"])
    G(lambda e: e.affine_select(out=ident[:], in_=ident[:], pattern=[[-1, 128]], compare_op=ALU.is_equal,
                                fill=0.0, base=0, channel_multiplier=1), ["ident"], ["ident"])
    G(lambda e: e.memset(ones[:], 1.0), [], ["ones"])
    V(lambda e: e.memset(st[:], 0.0), [], ["st"])
    V(lambda e: e.memset(T["st2"][:], 0.0), [], ["st2"])

    def bc_load(name, src, parts=128):
        P.dma("scalar", T[name][:], src.partition_broadcast(parts), [], [name])

    bc_load("g_attn_bc", g_attn)
    bc_load("g_ffn_bc", g_ffn)
    bc_load("ln_g_bc", ln_v_g)
    bc_load("ln_b_bc", ln_v_b)
    bc_load("goa_bc", g_out_a)
    bc_load("gob_bc", g_out_b, 64)
    bc_load("gq8_bc", g_q)
    bc_load("gk_bc", g_k)
    V(lambda e: e.tensor_scalar(out=T["gq8_bc"][:], in0=T["gq8_bc"][:], scalar1=0.125, scalar2=None, op0=ALU.mult),
      ["gq8_bc"], ["gq8_bc"])
    P.dma("sync", T["gob_col"][:], g_out_b.rearrange("(c p) -> p c", p=128), [], ["gob_col"],
          allow_slow_non_contiguous=True)

    WsT, WsTs = T["WsT"], T["WsTs"]
    P.dma("gpsimd", WsT[:], w_sT.rearrange("h j i -> j h i"), [], ["WsT"])
    G(lambda e: e.affine_select(out=WsT[:], in_=WsT[:], pattern=[[0, 8], [1, 128]], compare_op=ALU.is_ge,
                                fill=0.0, base=0, channel_multiplier=-1), ["WsT"], ["WsT"])
    wk = [("WsTs", b) for b in range(16)]
    G(lambda e: e.memset(WsTs[:], 0.0), [], wk)
    for b in range(16):
        P.dma("gpsimd", WsTs[4 * b:4 * b + 4, :, 4 * b:4 * b + 4],
              w_sT[:, 0:4, 0:4].rearrange("h j i -> j h i"), [], [("WsTs", b)], allow_slow_non_contiguous=True)
    G(lambda e: e.affine_select(out=WsTs[:], in_=WsTs[:], pattern=[[0, 8], [1, 64]], compare_op=ALU.is_ge,
                                fill=0.0, base=0, channel_multiplier=-1), wk, wk + ["WsTs"])
    bs32, bss32, bstmp, bsl, bsls, blockind = T["bs32"], T["bss32"], T["bstmp"], T["bsl"], T["bsls"], T["blockind"]
    P.dma("sync", bs32[:], b_s, [], ["bs32"])
    P.dma("sync", bss32[:].rearrange("h (s t) -> h s t", t=4), b_s[:, 0:4].unsqueeze(1).to_broadcast([8, 16, 4]),
          [], ["bss32"], allow_slow_non_contiguous=True)
    for (src, dst, w, sk, dk) in ((bs32, bsl, 128, "bs32", "bsl"), (bss32, bsls, 64, "bss32", "bsls")):
        V(lambda e, dst=dst: e.memset(dst[:], 0.0), [], [dk])
        V(lambda e, src=src, dst=dst, w=w: e.tensor_copy(out=dst[0:8, 0:w], in_=src[0:8, 0:w]), [sk], [dk])
        V(lambda e, src=src, dst=dst, w=w: e.tensor_tensor(out=bstmp[0:8, 0:w], in0=src[0:8, 0:w], in1=dst[0:8, 0:w],
                                                            op=ALU.subtract), [sk, dk], ["bstmp"])
        V(lambda e, dst=dst, w=w: e.tensor_copy(out=dst[32:40, 0:w], in_=bstmp[0:8, 0:w]), ["bstmp"], [dk])
    G(lambda e: e.memset(blockind[:], 0.0), [], ["blockind"])
    G(lambda e: e.memset(blockind[0:8, :], 1.0), ["blockind"], ["blockind"])
    G(lambda e: e.affine_select(out=blockind[0:8, :], in_=blockind[0:8, :], pattern=[[1, 512]], compare_op=ALU.is_ge,
                                fill=0.0, base=0, channel_multiplier=-64), ["blockind"], ["blockind"])
    G(lambda e: e.affine_select(out=blockind[0:8, :], in_=blockind[0:8, :], pattern=[[-1, 512]], compare_op=ALU.is_ge,
                                fill=0.0, base=63, channel_multiplier=64), ["blockind"], ["blockind"])
    ACOPY(blockind[32:40, :], blockind[0:8, :], ["blockind"], ["blockind"])
    mask2 = T["mask2"]
    G(lambda e: e.memset(mask2[:], 1.0), [], ["mask2"])
    G(lambda e: e.affine_select(out=mask2[:, 0:128], in_=mask2[:, 0:128], pattern=[[-1, 128]], compare_op=ALU.is_ge,
                                fill=0.0, base=0, channel_multiplier=1), ["mask2"], ["mask2"])
    G(lambda e: e.affine_select(out=mask2[:, 128:256], in_=mask2[:, 128:256], pattern=[[1, 128]], compare_op=ALU.is_ge,
                                fill=0.0, base=0, channel_multiplier=-1), ["mask2"], ["mask2"])
    selw, mb = T["selw"], T["mb"]
    G(lambda e: e.memset(selw[:], 0.0), [], ["selw"])
    G(lambda e: e.memset(selw[:, 63:64], 1.0), ["selw"], ["selw"])
    G(lambda e: e.memset(mb[:], 0.0), [], ["mb"])
    for t in range(1, 4):
        G(lambda e, t=t: e.affine_select(out=mb[:, t * 24:t * 24 + 8], in_=mb[:, t * 24:t * 24 + 8], pattern=[[0, 8]],
                                         compare_op=ALU.is_ge, fill=NEG, base=-t, channel_multiplier=1), ["mb"], ["mb"])

    Win, Wv, Wo, Wg, Wu, Wd = T["Win"], T["Wv"], T["Wo"], T["Wg"], T["Wu"], T["Wd"]
    w_in_v = w_in.rearrange("(k p) n -> p k n", p=128)
    wst = [T["wst0"], T["wst1"], T["wst2"]]
    for k in range(8):
        sb_ = wst[k % 3]
        P.dma("sync", sb_[:], w_in_v[:, k, :], [], ["wst%d" % (k % 3)])
        ACOPY(Win[:, k, 0:896], sb_[:, 0:896], ["wst%d" % (k % 3)], [("Win", k)])
        V(lambda e, k=k, sb_=sb_: e.tensor_copy(out=Win[:, k, 896:1920], in_=sb_[:, 896:1920]), ["wst%d" % (k % 3)], [("Winh", k)])
        G(lambda e, k=k, sb_=sb_: e.tensor_copy(out=Win[:, k, 1920:2560], in_=sb_[:, 1920:2560]), ["wst%d" % (k % 3)], [("Wing", k)])

    hT, hTs, qT, kT, mixTa, mixTb, mixTs = T["hT"], T["hTs"], T["qT"], T["kT"], T["mixTa"], T["mixTb"], T["mixTs"]
    xts = [T["xt0"], T["xt1"]]
    k32s = [T["k32_0"], T["k32_1"]]
    v32s = [T["v32_0"], T["v32_1"]]
    h_bf, u_sb, gv, wtmp, vn32, vn_bf, sq = T["h_bf"], T["u"], T["gv"], T["wtmp"], T["vn32"], T["vn_bf"], T["sq"]
    tq, q_bf, k_bf, a32, a_bf = T["tq"], T["q_bf"], T["k_bf"], T["a32"], T["a_bf"]
    ks32, vs32, qs_bf = T["ks32"], T["vs32"], T["qs_bf"]

    seq1 = [(True, 0)] + [(False, ti) for ti in range(NT)]
    NS1 = len(seq1)
    u2 = [T["u"], T["u_1"]]
    hb2 = [T["h_bf"], T["h_bf_1"]]
    vnb2 = [T["vn_bf"], T["vn_bf_1"]]
    qb2 = [T["q_bf"], T["q_bf_1"]]
    kb2 = [T["k_bf"], T["k_bf_1"]]
    ab2 = [T["a_bf"], T["a_bf_1"]]
    zq_sb, zk_sb = T["tq"], T["zk_sb"]

    def s1info(n):
        sample, ti = seq1[n]
        M = 64 if sample else 128
        par = n % 2
        cols = slice(0, 64) if sample else slice(ti * 128, (ti + 1) * 128)
        sb = n * 48
        return sample, ti, M, par, cols, sb, ("st", n)

    def s1_ld(n):
        sample, ti, M, par, cols, sb, stk = s1info(n)
        src = xs if sample else xp[ti * 128:(ti + 1) * 128, :]
        P.dma("sync", xts[par][0:M, :], src, [], ["xt%d" % par])

    def s1_fa(n):
        sample, ti, M, par, cols, sb, stk = s1info(n)
        xt, xk = xts[par], "xt%d" % par
        src = xs if sample else xp[ti * 128:(ti + 1) * 128, :]
        hb, hbk = hb2[par], "h_bf%d" % par
        sc0 = st[0:M, sb:sb + 1]
        ACT(sq[0:M, :], xt[0:M, :], AF.Square, [xk], ["sq", stk], accum_out=sc0)
        rstd_chain(sc0, 1.0 / D, stk)
        V(lambda e: e.scalar_tensor_tensor(out=hb[0:M, :], in0=xt[0:M, :], scalar=sc0, in1=T["g_attn_bc"][0:M, :],
                                           op0=ALU.mult, op1=ALU.mult), [xk, stk, "g_attn_bc"], [hbk])

    def s1_fb(n, part):
        sample, ti, M, par, cols, sb, stk = s1info(n)
        hb, hbk = hb2[par], "h_bf%d" % par
        hTd = hTs if sample else hT
        hk = "hTs" if sample else ("hT", ti)
        u_sb, uk = u2[par], "u%d" % par
        vnb, vnk = vnb2[par], "vn_bf%d" % par
        qbf, qk_ = qb2[par], "q_bf%d" % par
        kbf, kk_ = kb2[par], "k_bf%d" % par

        def sc(j, w=1):
            return st[0:M, sb + j:sb + j + w]

        if part == "tr":
            trb = bankbf(5)
            for k in range(8):
                TR(trb[:, k * 128:k * 128 + M], hb[0:M, k * 128:(k + 1) * 128], ident[0:M, 0:M], [hbk, "ident"], ["b5"],
                   signal=(k == 7))
            ACOPY(hTd[:, :, cols], trb.rearrange("p (k m) -> p k m", m=128)[:, :, 0:M], ["b5"], [hk])
            return
        if part == "proj":
            s1_proj(n)
            return
        s1_chain(n)

    def s1_proj(n):
        sample, ti, M, par, cols, sb, stk = s1info(n)
        hTd = hTs if sample else hT
        hk = "hTs" if sample else ("hT", ti)
        u_sb, uk = u2[par], "u%d" % par

        def sc(j, w=1):
            return st[0:M, sb + j:sb + j + w]

        for k in range(8):
            for c in range(5):
                MM(bank[c][0:M, :], hTd[:, k, cols], Win[:, k, c * 512:(c + 1) * 512], k == 0, k == 7,
                   [hk], ["b%d" % c])
        k32 = ks32 if sample else k32s[par]
        v32 = vs32 if sample else v32s[par]
        k32k = "ks32" if sample else "k32_%d" % par
        v32k = "vs32" if sample else "v32_%d" % par
        ACT(u_sb[0:M, :], bank[0][0:M, :], AF.Gelu_apprx_tanh, ["b0"], [uk])
        ACT(gv[0:M, :], bank[1][0:M, :], AF.Gelu_apprx_tanh, ["b1"], ["gv", stk], accum_out=sc(2))
        V(lambda e: e.tensor_copy(out=zq_sb[0:M, :], in_=bank[2][0:M, :]), ["b2"], ["zq_sb"])
        V(lambda e: e.tensor_copy(out=zk_sb[0:M, :], in_=bank[3][0:M, :]), ["b3"], ["zk_sb"])
        ACOPY(v32[0:M, :], bank[4][0:M, :], ["b4"], [v32k])

    def s1_chain(n):
        sample, ti, M, par, cols, sb, stk = s1info(n)
        vnb, vnk = vnb2[par], "vn_bf%d" % par
        qbf, qk_ = qb2[par], "q_bf%d" % par
        kbf, kk_ = kb2[par], "k_bf%d" % par
        k32 = ks32 if sample else k32s[par]
        v32 = vs32 if sample else v32s[par]
        k32k = "ks32" if sample else "k32_%d" % par
        v32k = "vs32" if sample else "v32_%d" % par

        def sc(j, w=1):
            return st[0:M, sb + j:sb + j + w]

        V(lambda e: e.tensor_scalar(out=sc(3), in0=sc(2), scalar1=-1.0 / 512, scalar2=None, op0=ALU.mult), [stk], [stk])
        ACT(sq[0:M, 0:512], gv[0:M, :], AF.Square, ["gv", stk], ["sq", stk], bias=sc(3), scale=1.0, accum_out=sc(7))
        ACT(sq[0:M, 512:1024], zq_sb[0:M, :], AF.Square, ["zq_sb"], ["sqq"])
        V(lambda e: e.reduce_sum(out=sc(8, 8), in_=sq[0:M, 512:1024].rearrange("p (h d) -> p h d", d=64), axis=AX.X),
          ["sqq"], [stk])
        ACT(sq[0:M, 512:1024], zk_sb[0:M, :], AF.Square, ["zk_sb"], ["sqq"])
        V(lambda e: e.reduce_sum(out=sc(16, 8), in_=sq[0:M, 512:1024].rearrange("p (h d) -> p h d", d=64), axis=AX.X),
          ["sqq"], [stk])
        V(lambda e: e.tensor_scalar(out=sc(7), in0=sc(7), scalar1=1.0 / 512, scalar2=EPS, op0=ALU.mult, op1=ALU.add), [stk], [stk])
        V(lambda e: e.tensor_scalar(out=sc(8, 16), in0=sc(8, 16), scalar1=1.0 / 64, scalar2=EPS, op0=ALU.mult, op1=ALU.add),
          [stk], [stk])
        ACT(sc(7, 17), sc(7, 17), AF.Sqrt, [stk], [stk])
        V(lambda e: e.reciprocal(out=sc(7, 17), in_=sc(7, 17)), [stk], [stk])
        V(lambda e: e.scalar_tensor_tensor(out=wtmp[0:M, :], in0=gv[0:M, :], scalar=sc(3), in1=T["ln_g_bc"][0:M, :],
                                           op0=ALU.add, op1=ALU.mult), ["gv", stk, "ln_g_bc"], ["wtmp"])
        if sample:
            V(lambda e: e.scalar_tensor_tensor(out=vn32[0:M, :], in0=wtmp[0:M, :], scalar=sc(7), in1=T["ln_b_bc"][0:M, :],
                                               op0=ALU.mult, op1=ALU.add), ["wtmp", stk, "ln_b_bc"], ["vn32"])
            V(lambda e: e.tensor_copy(out=vnb[0:M, :], in_=vn32[0:M, :]), ["vn32"], [vnk])
            P.dma("sync", nvc_s, vn32[0:M, :], ["vn32"], [])
        else:
            V(lambda e: e.scalar_tensor_tensor(out=vnb[0:M, :], in0=wtmp[0:M, :], scalar=sc(7), in1=T["ln_b_bc"][0:M, :],
                                               op0=ALU.mult, op1=ALU.add), ["wtmp", stk, "ln_b_bc"], [vnk])
        V(lambda e: e.tensor_tensor(out=zq_sb[0:M, :].rearrange("p (h d) -> p h d", d=64),
                                    in0=zq_sb[0:M, :].rearrange("p (h d) -> p h d", d=64),
                                    in1=sc(8, 8).unsqueeze(2).to_broadcast([M, 8, 64]), op=ALU.mult), ["zq_sb", stk], ["zq_sb"])
        V(lambda e: e.tensor_tensor(out=qbf[0:M, :].rearrange("p (h d) -> p h d", d=64),
                                    in0=zq_sb[0:M, :].rearrange("p (h d) -> p h d", d=64),
                                    in1=T["gq8_bc"][0:M, :].unsqueeze(1).to_broadcast([M, 8, 64]), op=ALU.mult),
          ["zq_sb", "gq8_bc"], [qk_])
        V(lambda e: e.tensor_tensor(out=zk_sb[0:M, :].rearrange("p (h d) -> p h d", d=64),
                                    in0=zk_sb[0:M, :].rearrange("p (h d) -> p h d", d=64),
                                    in1=sc(16, 8).unsqueeze(2).to_broadcast([M, 8, 64]), op=ALU.mult), ["zk_sb", stk], ["zk_sb"])
        V(lambda e: e.tensor_tensor(out=k32[0:M, :].rearrange("p (h d) -> p h d", d=64),
                                    in0=zk_sb[0:M, :].rearrange("p (h d) -> p h d", d=64),
                                    in1=T["gk_bc"][0:M, :].unsqueeze(1).to_broadcast([M, 8, 64]), op=ALU.mult),
          ["zk_sb", "gk_bc"], [k32k])
        if sample:
            P.dma("sync", nk_s, k32[0:M, :], [k32k], [])
            P.dma("sync", nv_s, v32[0:M, :], [v32k], [])
            V(lambda e: e.tensor_copy(out=qs_bf[:], in_=qbf[0:64, :]), [qk_], ["qs_bf"])
            P.dma("sync", qs_scr, qs_bf[:], ["qs_bf"], ["qs_scr"])
        else:
            P.dma("sync", nk_p[ti * 128:(ti + 1) * 128, :], k32[:], [k32k], [])
            P.dma("sync", nv_p[ti * 128:(ti + 1) * 128, :], v32[:], [v32k], [])
            G(lambda e: e.tensor_copy(out=kbf[:], in_=k32[:]), [k32k], [kk_])

    def s1_bk1(n):
        sample, ti, M, par, cols, sb, stk = s1info(n)
        u_sb, uk = u2[par], "u%d" % par
        vnb, vnk = vnb2[par], "vn_bf%d" % par
        qbf, qk_ = qb2[par], "q_bf%d" % par
        kbf, kk_ = kb2[par], "k_bf%d" % par
        abf, ak_ = ab2[par], "a_bf%d" % par
        sc6 = st[0:M, sb + 6:sb + 7]
        Wm, bl = (WsTs, bsls) if sample else (WsT, bsl)
        Wk = "WsTs" if sample else "WsT"
        for h in range(8):
            MM(bank[6][0:M, h * 64:(h + 1) * 64], Wm[0:M, h, 0:M], vnb[0:M, h * 64:(h + 1) * 64], h == 0, False,
               [Wk, vnk], ["b6"])
        MM(bank[6][0:M, :], bl[0:40, 0:M], blockind[0:40, :], False, True, ["bsls" if sample else "bsl", "blockind"], ["b6"])
        if DEBUG and not sample and ti == 0:
            pass

    def s1_bk1a(n):
        sample, ti, M, par, cols, sb, stk = s1info(n)
        u_sb, uk = u2[par], "u%d" % par
        abf, ak_ = ab2[par], "a_bf%d" % par
        sc6 = st[0:M, sb + 6:sb + 7]
        V(lambda e: e.tensor_tensor(out=a32[0:M, :], in0=u_sb[0:M, :], in1=bank[6][0:M, :], op=ALU.mult), [uk, "b6"], ["a32"])
        ACT(sq[0:M, 0:512], a32[0:M, :], AF.Square, ["a32"], ["sq", stk], accum_out=sc6)
        rstd_chain(sc6, 1.0 / 512, stk)
        V(lambda e: e.scalar_tensor_tensor(out=abf[0:M, :], in0=a32[0:M, :], scalar=sc6, in1=T["goa_bc"][0:M, :],
                                           op0=ALU.mult, op1=ALU.mult), ["a32", stk, "goa_bc"], [ak_])

    def s1_bk1b(n):
        sample, ti, M, par, cols, sb, stk = s1info(n)
        qbf, qk_ = qb2[par], "q_bf%d" % par
        kbf, kk_ = kb2[par], "k_bf%d" % par
        if not sample:
            t7 = bankbf(7)
            for j in range(4):
                TR(t7[:, j * 128:(j + 1) * 128], qbf[:, j * 128:(j + 1) * 128], ident[:], [qk_, "ident"], ["b7"], signal=False)
            for j in range(4):
                TR(t7[:, (4 + j) * 128:(5 + j) * 128], kbf[:, j * 128:(j + 1) * 128], ident[:], [kk_, "ident"], ["b7"], signal=(j == 3))
            t7v = t7.rearrange("p (k m) -> p k m", m=128)
            ACOPY(qT[:, :, cols], t7v[:, 0:4, :], ["b7"], [("qT", ti)])
            ACOPY(kT[:, :, cols], t7v[:, 4:8, :], ["b7"], [("kT", ti)])

    def s1_bk2(n):
        sample, ti, M, par, cols, sb, stk = s1info(n)
        abf, ak_ = ab2[par], "a_bf%d" % par
        t7 = bankbf(4)
        for j in range(4):
            TR(t7[:, j * 128:j * 128 + M], abf[0:M, j * 128:(j + 1) * 128], ident[0:M, 0:M], [ak_, "ident"], ["b4"], signal=(j == 3))
        mT = mixTs if sample else mixTa
        ACOPY(mT[:, 0:4, cols], t7.rearrange("p (k m) -> p k m", m=128)[:, 0:4, 0:M], ["b4"],
              ["mixTs_a" if sample else ("mixT_a", ti)])

    P.barrier()
    s1_ld(0)
    s1_ld(1)
    s1_fa(0)
    s1_fb(0, "tr")
    s1_fa(1)
    for i in range(NS1 + 2):
        if i + 2 < NS1:
            s1_ld(i + 2)
        if i < NS1:
            s1_fb(i, "proj")
        if i + 1 < NS1:
            s1_fb(i + 1, "tr")
        if 0 <= i - 2 < NS1:
            s1_bk2(i - 2)
        if 0 <= i - 1 < NS1:
            s1_bk1(i - 1)
            s1_bk1b(i - 1)
            s1_bk1a(i - 1)
        if i < NS1:
            s1_fb(i, "chain")
        if i + 2 < NS1:
            s1_fa(i + 2)
    P.barrier()

    accs = [T["accA"], T["accB"]]
    bT, ssqb, tmpd, sqb = T["bT"], T["ssqb"], T["tmpd"], T["sqb"]
    Vaug = [T["Vaug0"], T["Vaug1"], T["Vaug2"]]
    PT = [T["PT0"], T["PT1"], T["PT2"], T["PT3"]]
    P.dma("gpsimd", Wv[:], w_in_v[:, :, 2048:2560], [], ["Wv"])
    for i in range(3):
        G(lambda e, i=i: e.memset(Vaug[i][:], 1.0), [], ["Vaug%d" % i])
    w_o_v = w_o.rearrange("(k p) n -> p k n", p=128)
    for k2 in range(2):
        P.dma("gpsimd", Wo[:, 4 * k2:4 * k2 + 4, :], w_o_v[:, 4 * k2:4 * k2 + 4, :], [], [("Wo", k2)])

    def pos(dil, r, n):
        b0 = dil * 128 * n + r
        return slice(b0, b0 + 127 * dil + 1, dil)

    SB = [(1, 2), (5, 6)]
    OB = (3, 4)

    def item_front(w, it):
        pair, dil, r, n = it
        slot = w % 3
        kp = pos(dil, r, n)
        for k in range(8):
            MM(bank[0][:, 0:128], hT[:, k, kp], Wv[:, k, pair * 128:(pair + 1) * 128], k == 0, k == 7, ["Wv"], ["b0"])
        ACOPY(Vaug[slot][:].rearrange("p (a d) -> p a d", d=64)[:, 0:3:2, :],
              bank[0][:, 0:128].rearrange("p (a d) -> p a d", d=64), ["b0"], ["Vaug%d" % slot])
        if n > 0:
            kpp = pos(dil, r, n - 1)
            for hd in range(2):
                hb = 64 * hd
                bk = SB[w % 2][hd]
                MM(bank[bk][:, 0:128], kT[hb:hb + 64, pair, kpp], qT[hb:hb + 64, pair, kp], True, True, [], ["b%d" % bk],
                   signal=False)
        for hd in range(2):
            hb = 64 * hd
            bk = SB[w % 2][hd]
            MM(bank[bk][:, 128:256], kT[hb:hb + 64, pair, kp], qT[hb:hb + 64, pair, kp], n == 0, True, [], ["b%d" % bk])
        for hd in range(2):
            bk = SB[w % 2][hd]
            pt = PT[(w % 2) * 2 + hd]
            ptk = "PT%d" % ((w % 2) * 2 + hd)
            lo = 0 if n > 0 else 128
            ACT(pt[:, lo:256], bank[bk][:, lo:256], AF.Exp, ["b%d" % bk], [ptk])
            V(lambda e, pt=pt, lo=lo: e.tensor_tensor(out=pt[:, lo:256], in0=pt[:, lo:256], in1=mask2[:, lo:256], op=ALU.mult),
              [ptk, "mask2"], [ptk])

    def item_back(w, it):
        pair, dil, r, n = it
        slot = w % 3
        pslot = (w - 1) % 3
        qp = pos(dil, r, n)
        for hd in range(2):
            pt = PT[(w % 2) * 2 + hd]
            ptk = "PT%d" % ((w % 2) * 2 + hd)
            ob = OB[hd]
            hs = slice(0, 128) if hd == 0 else slice(64, 192)
            if n > 0:
                MM(bank[ob][:, 0:128], Vaug[pslot][:, hs], pt[:, 0:128], True, False, ["Vaug%d" % pslot, ptk], ["b%d" % ob])
            MM(bank[ob][:, 0:128], Vaug[slot][:, hs], pt[:, 128:256], n == 0, True, ["Vaug%d" % slot, ptk], ["b%d" % ob])
            acc = accs[hd]
            ak = "acc%d" % hd
            if dil == 1:
                ACOPY(acc[:, qp], bank[ob][:, 0:128], ["b%d" % ob], [ak])
            else:
                V(lambda e, acc=acc, ob=ob, qp=qp: e.tensor_tensor(out=acc[:, qp], in0=acc[:, qp], in1=bank[ob][:, 0:128],
                                                                   op=ALU.add), ["b%d" % ob, ak], [ak])

    for pair in range(4):
        items = []
        for dil in (1, 4, 16):
            for r in range(dil):
                for n in range(S // dil // 128):
                    items.append((pair, dil, r, n))
        for w, it in enumerate(items):
            item_front(w, it)
            if w > 0:
                item_back(w - 1, items[w - 1])
        item_back(len(items) - 1, items[-1])
        ACOPY(tmpd[0:64, :], accs[0][64:128, :], ["acc0"], ["tmpd"])
        ACOPY(tmpd[64:128, :], accs[1][0:64, :], ["acc1"], ["tmpd"])
        V(lambda e: e.reciprocal(out=tmpd[:], in_=tmpd[:]), ["tmpd"], ["tmpd"])
        V(lambda e, pair=pair: e.tensor_tensor(out=bT[0:64, pair, :], in0=accs[0][0:64, :], in1=tmpd[0:64, :], op=ALU.mult),
          ["acc0", "tmpd"], [("bT", pair)])
        V(lambda e, pair=pair: e.tensor_tensor(out=bT[64:128, pair, :], in0=accs[1][64:128, :], in1=tmpd[64:128, :], op=ALU.mult),
          ["acc1", "tmpd"], [("bT", pair)])
        ACT(sqb[:], bT[:, pair, :], AF.Square, [("bT", pair)], ["sqb"])
        for c in range(4):
            MM(bank[7][:, :], ones[:], sqb[:, c * 512:(c + 1) * 512], True, True, ["ones", "sqb"], ["b7"])
            if pair == 0:
                V(lambda e, c=c: e.tensor_copy(out=ssqb[:, c * 512:(c + 1) * 512], in_=bank[7][:, :]), ["b7"], ["ssqb"])
            else:
                V(lambda e, c=c: e.tensor_tensor(out=ssqb[:, c * 512:(c + 1) * 512], in0=ssqb[:, c * 512:(c + 1) * 512],
                                                 in1=bank[7][:, :], op=ALU.add), ["b7", "ssqb"], ["ssqb"])
    rstd_chain(ssqb[:], 1.0 / 512, "ssqb")
    for pair in range(4):
        V(lambda e, pair=pair: e.scalar_tensor_tensor(out=mixTb[:, pair, :], in0=bT[:, pair, :],
                                                      scalar=T["gob_col"][:, pair:pair + 1], in1=ssqb[:],
                                                      op0=ALU.mult, op1=ALU.mult),
          [("bT", pair), "ssqb", "gob_col"], [("mixT_b", pair)])
    P.barrier()
    if DEBUG:
        P.dma("sync", d_mixTa, mixTa[:], [], [])
        P.dma("sync", d_mixTb, mixTb[:], [], [])
        P.barrier()

    wg_v = w_gate.rearrange("(k p) n -> p k n", p=128)
    wu_v = w_up.rearrange("(k p) n -> p k n", p=128)
    wd_v = w_down.rearrange("(f p) n -> p f n", p=128)
    wdma = []
    fst = [T["fst0"], T["fst1"]]
    jj = 0
    pend = []
    for blk in range(6):
        c0, c1 = blk * 512, min(DFF, blk * 512 + 512)
        for (Wt, wv_, nm) in ((Wg, wg_v, "Wg"), (Wu, wu_v, "Wu")):
            sb_ = fst[jj % 2]
            wdma.append(P.dma("scalar", sb_[:, :, 0:c1 - c0], wv_[:, :, c0:c1], [], ["fst%d" % (jj % 2)]))
            pend.append((Wt, sb_, c0, c1, nm, blk, jj % 2))
            if len(pend) == 2:
                Wt_, sb2, a0, a1, nm_, blk_, par_ = pend.pop(0)
                ACOPY(Wt_[:, :, a0:a1], sb2[:, :, 0:a1 - a0], ["fst%d" % par_], [(nm_, blk_)])
            jj += 1
    while pend:
        Wt_, sb2, a0, a1, nm_, blk_, par_ = pend.pop(0)
        ACOPY(Wt_[:, :, a0:a1], sb2[:, :, 0:a1 - a0], ["fst%d" % par_], [(nm_, blk_)])

    xas = [T["xa%d" % i] for i in range(4)]

    def wo_load(ti):
        P.dma("sync", xas[ti % 4][:], xp[ti * 128:(ti + 1) * 128, :], [], ["xa%d" % (ti % 4)])
    OBK = [(0, 1), (2, 3)]

    def wo_tile(ti, sample):
        M = 64 if sample else 128
        par = ti % 2
        xa = T["xio0"] if sample else xas[ti % 4]
        xk = "xio0" if sample else "xa%d" % (ti % 4)
        WoT = T["Wo7"] if sample else Wo
        src = xs if sample else xp[ti * 128:(ti + 1) * 128, :]
        dst = y_s if sample else y_p[ti * 128:(ti + 1) * 128, :]
        cols = slice(0, 64) if sample else slice(ti * 128, (ti + 1) * 128)
        if sample:
            P.dma("sync", xa[0:M, :], src, [], [xk])
        for k in range(8):
            mTk = mixTs[:, k, cols] if sample else (mixTa[:, k, cols] if k < 4 else mixTb[:, k - 4, cols])
            for c in range(2):
                bk = OBK[par][c]
                MM(bank[bk][0:M, :], mTk, WoT[:, k, c * 512:(c + 1) * 512], k == 0, k == 7,
                   [("Wo7" if sample else "Wo", k // 4)] + (["mixTs_a", "mixTs_b"] if sample else []), ["b%d" % bk])
        for c in range(2):
            bk = OBK[par][c]
            V(lambda e, c=c, bk=bk: e.tensor_tensor(out=xa[0:M, c * 512:(c + 1) * 512], in0=xa[0:M, c * 512:(c + 1) * 512],
                                                    in1=bank[bk][0:M, :], op=ALU.add), [xk, "b%d" % bk], [xk])
        P.dma("sync" if sample else "gpsimd", dst, xa[0:M, :], [xk], [("x1", 16 if sample else ti)])
        if not sample:
            col = 8 + ti
            ACT(T["junk3"][0:M, :], xa[0:M, :], AF.Square, [xk], ["junk3", "st2"], accum_out=T["st2"][0:M, col:col + 1])
            rstd_chain(T["st2"][0:M, col:col + 1], 1.0 / D, "st2")

    for ti in range(3):
        wo_load(ti)
    for ti in range(NT):
        if ti + 3 < NT:
            wo_load(ti + 3)
        wo_tile(ti, False)
    P.barrier(skip=wdma)
    for blk in range(6):
        f0, f1 = blk * 4, min(NF, blk * 4 + 4)
        P.dma("gpsimd", Wd[:, f0:f1, :], wd_v[:, f0:f1, :], [], [("Wd", blk)])

    Kt = [T["Kt0"], T["Kt1"]]
    Vt = [T["Vt0"], T["Vt1"]]
    qb = [T["qb0"], T["qb1"]]
    wV = [T["wV0"], T["wV1"]]
    pbf = [T["pbf0"], T["pbf1"], T["pbf2"], T["pbf3"]]
    prod, s24 = T["prod"], T["s24"]
    samp_state = {"n": 0}

    def samp_rows(t, b):
        if b == 0:
            return slice(1920, 2048, 1)
        if b == 1:
            return slice(1536 + t, 2048, 4)
        return slice(t, 2048, 16)

    def samp_pf_k(idx):
        if idx >= 64:
            return
        sq_, t = idx // 4, idx % 4
        sl = idx % 2
        for b in range(3):
            P.dma("gpsimd", Kt[sl][:, b, :], ck[sq_, samp_rows(t, b), :], [], [("Kt", sl, b)])
        P.dma("sync", qb[sl][:], qs_scr[idx, :].partition_broadcast(128), ["qs_scr"], ["qb%d" % sl])

    def samp_pf_v(idx):
        if idx >= 64:
            return
        sq_, t = idx // 4, idx % 4
        sl = idx % 2
        for b in range(3):
            P.dma("gpsimd", Vt[sl][:, b, :], cv[sq_, samp_rows(t, b), :], [], [("Vt", sl, b)])

    s24s = [T["s24"], T["s24_1"]]

    def samp_scores(idx):
        sq_, t = idx // 4, idx % 4
        sl = idx % 2
        s24_ = s24s[idx % 2]
        sk = "s24_%d" % (idx % 2)
        for b in range(3):
            V(lambda e, b=b: e.tensor_tensor(out=prod[:], in0=Kt[sl][:, b, :], in1=qb[sl][:], op=ALU.mult),
              [("Kt", sl, b), "qb%d" % sl], ["prod"])
            V(lambda e, b=b: e.reduce_sum(out=s24_[:, b * 8:(b + 1) * 8], in_=prod[:].rearrange("p (h d) -> p h d", d=64),
                                          axis=AX.X), ["prod"], [sk])
        if t > 0:
            V(lambda e: e.tensor_tensor(out=s24_[:], in0=s24_[:], in1=mb[:, t * 24:(t + 1) * 24], op=ALU.add), [sk, "mb"], [sk])

    def samp_exp(idx):
        p4 = idx % 4
        ACT(pbf[p4][:], s24s[idx % 2][:], AF.Exp, ["s24_%d" % (idx % 2)], ["pbf%d" % p4])

    def samp_wv(idx):
        sl = idx % 2
        p4 = idx % 4
        G(lambda e: e.tensor_tensor(out=wV[sl][:, 0:2, :].rearrange("p b (h d) -> p (b h) d", d=64),
                                    in0=Vt[sl][:, 0:2, :].rearrange("p b (h d) -> p (b h) d", d=64),
                                    in1=pbf[p4][:, 0:16].unsqueeze(2).to_broadcast([128, 16, 64]), op=ALU.mult),
          [("Vt", sl, 0), ("Vt", sl, 1), "pbf%d" % p4], [("wV", sl, 0)])
        V(lambda e: e.tensor_tensor(out=wV[sl][:, 2, :].rearrange("p (h d) -> p h d", d=64),
                                    in0=Vt[sl][:, 2, :].rearrange("p (h d) -> p h d", d=64),
                                    in1=pbf[p4][:, 16:24].unsqueeze(2).to_broadcast([128, 8, 64]), op=ALU.mult),
          [("Vt", sl, 2), "pbf%d" % p4], [("wV", sl, 1)])

    def samp_back(idx):
        sl = idx % 2
        p4 = idx % 4
        sel = selw[:, 63 - idx:127 - idx]
        for b in range(3):
            first = (idx == 0 and b == 0)
            last = (idx == 63 and b == 2)
            MM(bank[6][0:64, :], sel, wV[sl][:, b, :], first, last, ["selw", ("wV", sl, 0), ("wV", sl, 1)], ["b6"])
            MM(bank[7][0:64, 0:8], sel, pbf[p4][:, b * 8:(b + 1) * 8], first, last, ["selw", "pbf%d" % p4], ["b7"],
               signal=(b == 2))

    NSTEP = 67

    def samp_step():
        i = samp_state["n"]
        if i >= NSTEP:
            return
        if i == 0:
            samp_pf_k(0)
            samp_pf_k(1)
        if 1 <= i <= 64:
            samp_exp(i - 1)
        if 2 <= i <= 65:
            samp_wv(i - 2)
        samp_pf_v(i)
        if i < 64:
            samp_scores(i)
        samp_pf_k(i + 2)
        if 3 <= i <= 66:
            samp_back(i - 3)
        samp_state["n"] = i + 1

    xios = [T["xio0"], T["xio1"]]
    h2_bf, h2T, actT = T["h2_bf"], T["h2T"], T["actT"]
    sils = [T["sil0"], T["sil1"]]

    def ffn_prep_a(g, j, sample):
        M = 64 if sample else 128
        ti = 16 if sample else g * 4 + j
        xio = xios[j % 2]
        xk = "xio%d" % (j % 2)
        src = y_s if sample else y_p[ti * 128:(ti + 1) * 128, :]
        col = 8 + ti
        P.dma("sync", xio[0:M, :], src, [("x1", ti)], [xk])
        if sample:
            ACT(h2_bf[0:M, :], xio[0:M, :], AF.Square, [xk], ["h2_bf", "st2"], accum_out=T["st2"][0:M, col:col + 1])
            rstd_chain(T["st2"][0:M, col:col + 1], 1.0 / D, "st2")
        V(lambda e: e.scalar_tensor_tensor(out=h2_bf[0:M, :], in0=xio[0:M, :], scalar=T["st2"][0:M, col:col + 1],
                                           in1=T["g_ffn_bc"][0:M, :], op0=ALU.mult, op1=ALU.mult),
          [xk, "st2", "g_ffn_bc"], ["h2_bf"])

    def ffn_prep_b(g, j, sample):
        M = 64 if sample else 128
        h2T_ = T["h2Ts"] if sample else h2T
        t0b = bankbf(0)
        for k in range(8):
            TR(t0b[:, k * 128:k * 128 + M], h2_bf[0:M, k * 128:(k + 1) * 128], ident[0:M, 0:M], ["h2_bf", "ident"], ["b0"], signal=(k == 7))
        ACOPY(h2T_[:, :, j * 128:j * 128 + M], t0b.rearrange("p (k m) -> p k m", m=128)[:, :, 0:M], ["b0"], ["h2T"])

    def ffn_gateup(g, sample):
        NTOK = 64 if sample else 512
        h2T_ = T["h2Ts"] if sample else h2T
        actT_ = T["actTs"] if sample else actT
        for f in range(NF):
            gb, ub = (0, 2) if f % 2 == 0 else (1, 3)
            for k in range(8):
                MM(bank[gb][:, 0:NTOK], Wg[:, k, f * 128:(f + 1) * 128], h2T_[:, k, 0:NTOK], k == 0, k == 7,
                   ["h2T", ("Wg", f // 4)], ["b%d" % gb])
            for k in range(8):
                MM(bank[ub][:, 0:NTOK], Wu[:, k, f * 128:(f + 1) * 128], h2T_[:, k, 0:NTOK], k == 0, k == 7,
                   ["h2T", ("Wu", f // 4)], ["b%d" % ub])
            sl = sils[f % 2]
            ACT(sl[:, 0:NTOK], bank[gb][:, 0:NTOK], AF.Silu, ["b%d" % gb], ["sil%d" % (f % 2)])
            V(lambda e, f=f, sl=sl, ub=ub: e.tensor_tensor(out=actT_[:, f, 0:NTOK], in0=sl[:, 0:NTOK], in1=bank[ub][:, 0:NTOK],
                                                           op=ALU.mult), ["sil%d" % (f % 2), "b%d" % ub], [("actT", f)])
            if not sample and f % 2 == 1:
                samp_step()

    def ffn_down_mm(g, j, sample):
        M = 64 if sample else 128
        actT_ = T["actTs"] if sample else actT
        bks = (4, 5) if j % 2 == 0 else (2, 3)
        for c in range(2):
            bk = bks[c]
            for f in range(NF):
                MM(bank[bk][0:M, :], actT_[:, f, j * 128:j * 128 + M], Wd[:, f, c * 512:(c + 1) * 512], f == 0, f == NF - 1,
                   [("actT", f), ("Wd", f // 4)], ["b%d" % bk])

    def ffn_down_fin(g, j, sample):
        M = 64 if sample else 128
        ti = 16 if sample else g * 4 + j
        xio = xios[j % 2]
        xk = "xio%d" % (j % 2)
        src = y_s if sample else y_p[ti * 128:(ti + 1) * 128, :]
        bks = (4, 5) if j % 2 == 0 else (2, 3)
        P.dma("sync", xio[0:M, :], src, [("x1", ti)], [xk])
        for c in range(2):
            bk = bks[c]
            V(lambda e, c=c, bk=bk: e.tensor_tensor(out=xio[0:M, c * 512:(c + 1) * 512], in0=xio[0:M, c * 512:(c + 1) * 512],
                                                    in1=bank[bk][0:M, :], op=ALU.add), [xk, "b%d" % bk], [xk])
        P.dma("sync", src, xio[0:M, :], [xk], [("yout", ti)])

    for j in range(4):
        ffn_prep_a(0, j, False)
        ffn_prep_b(0, j, False)
        samp_step()
    for g in range(4):
        ffn_gateup(g, False)
        if g < 3:
            ffn_prep_a(g + 1, 0, False)
        for j in range(4):
            ffn_down_mm(g, j, False)
            if g < 3:
                ffn_prep_b(g + 1, j, False)
                if j < 3:
                    ffn_prep_a(g + 1, j + 1, False)
            ffn_down_fin(g, j, False)
            samp_step()
    while samp_state["n"] < NSTEP:
        samp_step()

    P.barrier()
    for k2 in range(2):
        P.dma("gpsimd", T["Wo7"][:, 4 * k2:4 * k2 + 4, :], w_o_v[:, 4 * k2:4 * k2 + 4, :], [], [("Wo7", k2)])
    ksh, vsh, sprod, snum, sb_bf, qs6 = T["ksh"], T["vsh"], T["sprod"], T["snum"], T["sb_bf"], T["qs6"]
    st2 = T["st2"]
    V(lambda e: e.memset(ksh[:], 0.0), [], [("ksh", j) for j in range(4)])
    V(lambda e: e.memset(vsh[:], 0.0), [], [("vsh", j) for j in range(4)])
    P.dma("sync", qs6[:], qs_scr, [], ["qs6"])
    for j in range(0, 4):
        P.dma("sync", ksh[j:64, j, :], nk_s[0:64 - j, :], [], [("ksh", j)])
        P.dma("sync", vsh[j:64, j, :], nv_s[0:64 - j, :], [], [("vsh", j)])
    tcol = st2[0:64, 30:31]
    bself = st2[0:64, 32:64]
    trow_i, trow_f = T["trow_i"], T["trow_f"]
    G(lambda e: e.iota(out=trow_i[:].rearrange("o (s t) -> o s t", t=4), pattern=[[0, 16], [1, 4]], base=0,
                       channel_multiplier=0), [], ["trow_i"])
    V(lambda e: e.tensor_copy(out=trow_f[:], in_=trow_i[:]), ["trow_i"], ["trow_f"])
    P.dma("sync", tscr.rearrange("(o n) -> o n", o=1), trow_f[:], ["trow_f"], ["tscr"])
    P.dma("sync", tcol, tscr.rearrange("(p o) -> p o", o=1), ["tscr", "st2"], ["st2"])
    V(lambda e: e.memset(bself[:, 0:8], float(np.log(3.0))), ["st2"], ["st2"])
    for j in range(1, 4):
        V(lambda e, j=j: e.tensor_scalar(out=bself[:, j * 8:(j + 1) * 8], in0=tcol.to_broadcast([64, 8]), scalar1=float(j),
                                         scalar2=None, op0=ALU.is_ge), ["st2"], ["st2"])
        V(lambda e, j=j: e.tensor_scalar(out=bself[:, j * 8:(j + 1) * 8], in0=bself[:, j * 8:(j + 1) * 8], scalar1=-1.0,
                                         scalar2=-NEG, op0=ALU.add, op1=ALU.mult), ["st2"], ["st2"])
    ss = prod[0:64, 0:32]
    ee = prod[0:64, 64:96]
    for j in range(4):
        kin = ksh[:, j, :]
        V(lambda e, kin=kin: e.tensor_tensor(out=sprod[:], in0=qs6[:], in1=kin, op=ALU.mult), ["qs6", ("ksh", j)], ["sprod"])
        V(lambda e, j=j: e.reduce_sum(out=ss[:, j * 8:(j + 1) * 8], in_=sprod[:].rearrange("p (h d) -> p h d", d=64), axis=AX.X),
          ["sprod"], ["prod"])
    V(lambda e: e.tensor_tensor(out=ss, in0=ss, in1=bself, op=ALU.add), ["prod", "st2"], ["prod"])
    ACT(ee, ss, AF.Exp, ["prod"], ["prod"])
    V(lambda e: e.tensor_copy(out=snum[:], in_=bank[6][0:64, :]), ["b6"], ["snum"])
    for j in range(4):
        vin = vsh[:, j, :]
        V(lambda e, j=j, vin=vin: e.tensor_tensor(out=sprod[:].rearrange("p (h d) -> p h d", d=64),
                                                  in0=vin.rearrange("p (h d) -> p h d", d=64),
                                                  in1=ee[:, j * 8:(j + 1) * 8].unsqueeze(2).to_broadcast([64, 8, 64]),
                                                  op=ALU.mult), [("vsh", j), "prod"], ["sprod"])
        V(lambda e: e.tensor_tensor(out=snum[:], in0=snum[:], in1=sprod[:], op=ALU.add), ["snum", "sprod"], ["snum"])
    den = st2[0:64, 16:24]
    V(lambda e: e.tensor_copy(out=den, in_=bank[7][0:64, 0:8]), ["b7"], ["st2"])
    for j in range(4):
        V(lambda e, j=j: e.tensor_tensor(out=den, in0=den, in1=ee[:, j * 8:(j + 1) * 8], op=ALU.add), ["st2", "prod"], ["st2"])
    V(lambda e: e.reciprocal(out=den, in_=den), ["st2"], ["st2"])
    V(lambda e: e.tensor_tensor(out=snum[:].rearrange("p (h d) -> p h d", d=64), in0=snum[:].rearrange("p (h d) -> p h d", d=64),
                                in1=den.unsqueeze(2).to_broadcast([64, 8, 64]), op=ALU.mult), ["snum", "st2"], ["snum"])
    ACT(sprod[:], snum[:], AF.Square, ["snum"], ["sprod", "st2"], accum_out=st2[0:64, 26:27])
    rstd_chain(st2[0:64, 26:27], 1.0 / 512, "st2")
    V(lambda e: e.scalar_tensor_tensor(out=sb_bf[:], in0=snum[:], scalar=st2[0:64, 26:27], in1=T["gob_bc"][:],
                                       op0=ALU.mult, op1=ALU.mult), ["snum", "st2", "gob_bc"], ["sb_bf"])
    t5 = bankbf(5)
    for j in range(4):
        TR(t5[:, j * 128:j * 128 + 64], sb_bf[:, j * 128:(j + 1) * 128], ident[0:64, 0:64], ["sb_bf", "ident"], ["b5"])
    ACOPY(mixTs[:, 4:8, :], t5.rearrange("p (k m) -> p k m", m=128)[:, 0:4, 0:64], ["b5"], ["mixTs_b"])
    P.barrier()
    if DEBUG:
        P.dma("sync", d_mixTs, mixTs[:], [], [])
    wo_tile(16, True)
    ffn_prep_a(4, 0, True)
    ffn_prep_b(4, 0, True)
    ffn_gateup(4, True)
    ffn_down_mm(4, 0, True)
    ffn_down_fin(4, 0, True)

    P.finish()
    P.emit()
    return nc, P


_CACHE = {}


def kernel(x_prompt, x_sample, cache_k, cache_v, g_attn, w_in, ln_v_g, ln_v_b, w_s, b_s, g_q, g_k,
           g_out_a, g_out_b, w_o, g_ffn, w_gate, w_up, w_down):
    f = lambda a: np.ascontiguousarray(np.asarray(a, dtype=np.float32))
    if "nc" not in _CACHE:
        _CACHE["nc"] = build_program()[0]
    nc = _CACHE["nc"]
    shared = {
        "g_attn": f(g_attn[0]), "w_in": f(w_in[0]), "ln_v_g": f(ln_v_g[0]), "ln_v_b": f(ln_v_b[0]),
        "w_sT": f(np.transpose(np.asarray(w_s[0]), (0, 2, 1))), "b_s": f(b_s[0]), "g_q": f(g_q[0]), "g_k": f(g_k[0]),
        "g_out_a": f(g_out_a[0]), "g_out_b": f(g_out_b[0]), "w_o": f(w_o[0]), "g_ffn": f(g_ffn[0]),
        "w_gate": f(w_gate[0]), "w_up": f(w_up[0]), "w_down": f(w_down[0]),
    }
    xpn, xsn = np.asarray(x_prompt), np.asarray(x_sample)
    ckn, cvn = np.asarray(cache_k), np.asarray(cache_v)
    in_maps = []
    for c in range(8):
        m = dict(shared)
        m["xp"] = f(xpn[c])
        m["xs"] = f(xsn[16 * c:16 * c + 16].reshape(64, D))
        m["ck"] = f(ckn[0, 16 * c:16 * c + 16].reshape(16, 2048, 512))
        m["cv"] = f(cvn[0, 16 * c:16 * c + 16].reshape(16, 2048, 512))
        in_maps.append(m)
    res = run_bass_kernel_spmd(nc, in_maps, core_ids=list(range(8))).results
    cat = lambda k: np.stack([np.asarray(r[k], dtype=np.float32) for r in res], 0)
    y_prompt = cat("y_p")
    y_sample = cat("y_s").reshape(128, 4, D)
    nkp = cat("nk_p").reshape(1, 8, S, 8, 64)
    nvp = cat("nv_p").reshape(1, 8, S, 8, 64)
    nks = cat("nk_s").reshape(1, 128, 4, 8, 64)
    nvs = cat("nv_s").reshape(1, 128, 4, 8, 64)
    nvc = cat("nvc_s").reshape(1, 128, 4, 512)
    return (y_prompt, y_sample, nkp, nvp, nks, nvs, nvc)
```

```python
import numpy as np
import concourse.bass as bass
import concourse.mybir as mybir
from concourse.bass_utils import run_bass_kernel_spmd

F32 = mybir.dt.float32
BF16 = mybir.dt.bfloat16
I32 = mybir.dt.int32
AF = mybir.ActivationFunctionType
ALU = mybir.AluOpType
AX = mybir.AxisListType

ENGS = ("sync", "scalar", "gpsimd", "vector", "tensor")
D = 1024
S = 2048
NT = 16
DFF = 2816
NF = 22
EPS = 1e-6
NEG = -30000.0


class Prog:
    def __init__(self, nc, n_dma_sems=40):
        self.nc = nc
        self.ops = {e: [] for e in ENGS}
        self.cnt = {e: 0 for e in ENGS}
        self.sem = {e: nc.alloc_semaphore("prog_" + e) for e in ENGS}
        self.seen = {e: {} for e in ENGS}
        self.dma_sems = [nc.alloc_semaphore("dmas%d" % i) for i in range(n_dma_sems)]
        self.dma_cnt = [0] * n_dma_sems
        self.dma_rr = 0
        self.last_w = {}
        self.readers = {}
        self.n_inst = 0
        self.pending = {e: False for e in ENGS}

    def _wait(self, eng, ev):
        if ev[0] == "e":
            if ev[1] == eng and eng == "tensor":
                return
            key, sem, val = ev[1], self.sem[ev[1]], ev[2]
        else:
            key, sem, val = ("d", ev[1]), self.dma_sems[ev[1]], ev[2]
        if self.seen[eng].get(key, 0) >= val:
            return
        self.seen[eng][key] = val
        self.ops[eng].append(lambda e, sem=sem, val=val: e.wait_ge(sem, val))
        self.n_inst += 1

    def _deps(self, eng, reads, writes, waits):
        for ev in waits:
            self._wait(eng, ev)
        for k in reads:
            ev = self.last_w.get(k)
            if ev is not None:
                self._wait(eng, ev)
        for k in writes:
            ev = self.last_w.get(k)
            if ev is not None:
                self._wait(eng, ev)
            for ev in self.readers.get(k, ()):
                self._wait(eng, ev)

    def _commit(self, ev, reads, writes):
        for k in reads:
            self.readers.setdefault(k, []).append(ev)
        for k in writes:
            self.last_w[k] = ev
            self.readers[k] = []

    def op(self, eng, fn, reads=(), writes=(), waits=(), signal=True):
        self._deps(eng, reads, writes, waits)
        if signal:
            self.cnt[eng] += 1
            ev = ("e", eng, self.cnt[eng])
            sem = self.sem[eng]
            self.ops[eng].append(lambda e, fn=fn, sem=sem: fn(e).then_inc(sem, 1))
            self.pending[eng] = False
        else:
            ev = ("e", eng, self.cnt[eng] + 1)
            self.ops[eng].append(lambda e, fn=fn: fn(e))
            self.pending[eng] = True
        self.n_inst += 1
        self._commit(ev, reads, writes)
        return ev

    def dma(self, eng, out, in_, reads=(), writes=(), waits=(), **kw):
        self._deps(eng, reads, writes, waits)
        i = self.dma_rr
        self.dma_rr = (self.dma_rr + 1) % len(self.dma_sems)
        if self.dma_cnt[i] > 0:
            self._wait(eng, ("d", i, self.dma_cnt[i]))
        self.dma_cnt[i] += 16
        ev = ("d", i, self.dma_cnt[i])
        sem = self.dma_sems[i]
        self.ops[eng].append(
            lambda e, out=out, in_=in_, sem=sem, kw=kw: e.dma_start(out=out, in_=in_, **kw).then_inc(sem, 16))
        self.n_inst += 1
        self._commit(ev, reads, writes)
        return ev

    def barrier(self, skip=()):
        skipset = set(skip)
        assert not any(self.pending.values()), "non-signalled op pending at barrier"
        for e in ENGS:
            for o in ENGS:
                if o != e and self.cnt[o] > 0:
                    self._wait(e, ("e", o, self.cnt[o]))
            for i, c in enumerate(self.dma_cnt):
                while c > 0 and ("d", i, c) in skipset:
                    c -= 16
                if c > 0:
                    self._wait(e, ("d", i, c))

    def finish(self):
        for i, c in enumerate(self.dma_cnt):
            if c > 0:
                self._wait("sync", ("d", i, c))
        for o in ENGS:
            if o != "sync" and self.cnt[o] > 0:
                self._wait("sync", ("e", o, self.cnt[o]))

    def emit(self):
        with self.nc.Block() as block:
            for name in ENGS:
                ops = self.ops[name]

                def body(e, ops=ops):
                    for f in ops:
                        f(e)
                getattr(block, name)(body)


class Arena:
    LO = 16512
    HI = 229344

    def __init__(self, nc):
        self.nc = nc
        self.items = []
        self.t = {}

    def decl(self, name, shape, dtype, p0, p1):
        esz = 4 if dtype in (F32, I32) else 2
        n = 1
        for s in shape[1:]:
            n *= s
        nbytes = (n * esz + 31) // 32 * 32
        self.items.append((name, list(shape), dtype, nbytes, p0, p1))

    def build(self):
        placed = []
        order = sorted(self.items, key=lambda it: -it[3])
        for name, shape, dtype, nbytes, p0, p1 in order:
            cands = [self.LO] + sorted(e for (_, e, _, _) in placed)
            off = None
            for c in cands:
                ok = True
                for (o2, e2, q0, q1) in placed:
                    if not (p1 < q0 or q1 < p0) and not (c + nbytes <= o2 or e2 <= c):
                        ok = False
                        break
                if ok and c + nbytes <= self.HI:
                    off = c
                    break
            if off is None:
                raise RuntimeError("SBUF plan failed for %s (%d bytes, phases %d-%d)" % (name, nbytes, p0, p1))
            placed.append((off, off + nbytes, p0, p1))
            self.t[name] = self.nc.alloc_sbuf_tensor_at(name, shape, dtype, offset=off)
        return self.t


DEBUG = False


def build_program():
    nc = bass.Bass("TRN2", target_bir_lowering=False)
    P = Prog(nc)

    def din(name, shape, dt=F32):
        return nc.dram_tensor(name, list(shape), dt, kind="ExternalInput").ap()

    def dout(name, shape, dt=F32):
        return nc.dram_tensor(name, list(shape), dt, kind="ExternalOutput").ap()

    xp = din("xp", [S, D])
    xs = din("xs", [64, D])
    ck = din("ck", [16, 2048, 512])
    cv = din("cv", [16, 2048, 512])
    g_attn = din("g_attn", [D])
    w_in = din("w_in", [D, 2560])
    ln_v_g = din("ln_v_g", [512])
    ln_v_b = din("ln_v_b", [512])
    w_sT = din("w_sT", [8, 128, 128])
    b_s = din("b_s", [8, 128])
    g_q = din("g_q", [64])
    g_k = din("g_k", [64])
    g_out_a = din("g_out_a", [512])
    g_out_b = din("g_out_b", [512])
    w_o = din("w_o", [D, D])
    g_ffn = din("g_ffn", [D])
    w_gate = din("w_gate", [D, DFF])
    w_up = din("w_up", [D, DFF])
    w_down = din("w_down", [DFF, D])

    y_p = dout("y_p", [S, D])
    y_s = dout("y_s", [64, D])
    nk_p = dout("nk_p", [S, 512])
    nv_p = dout("nv_p", [S, 512])
    nk_s = dout("nk_s", [64, 512])
    nv_s = dout("nv_s", [64, 512])
    nvc_s = dout("nvc_s", [64, 512])
    qs_scr = dout("qs_scr", [64, 512], BF16)
    tscr = dout("tscr", [64])
    if DEBUG:
        d_mixTa = dout("d_mixTa", [128, 4, S], BF16)
        d_mixTb = dout("d_mixTb", [128, 4, S], BF16)
        d_mixTs = dout("d_mixTs", [128, 8, 64], BF16)
        d_u = dout("d_u", [128, 512])
        d_a32 = dout("d_a32", [128, 512])
        d_s = dout("d_s", [128, 512])
        d_vnbf = dout("d_vnbf", [128, 512], BF16)
        d_WsT = dout("d_WsT", [128, 8, 128], BF16)
        d_bsl = dout("d_bsl", [40, 128], BF16)
        d_blockind = dout("d_blockind", [40, 512], BF16)

    A = Arena(nc)
    A.decl("ident", [128, 128], BF16, 0, 5)
    A.decl("ones", [128, 128], BF16, 0, 2)
    A.decl("g_ffn_bc", [128, D], F32, 0, 5)
    A.decl("gob_bc", [64, 512], F32, 0, 5)
    A.decl("gob_col", [128, 4], F32, 0, 2)
    A.decl("st", [128, 17 * 48], F32, 0, 1)
    A.decl("st2", [128, 64], F32, 0, 7)
    A.decl("selw", [128, 127], BF16, 0, 7)
    A.decl("mb", [128, 4 * 24], F32, 0, 7)
    A.decl("ks32", [64, 512], F32, 1, 1)
    A.decl("vs32", [64, 512], F32, 1, 1)
    A.decl("qs_bf", [64, 512], BF16, 1, 1)
    A.decl("mixTs", [128, 8, 64], BF16, 0, 7)
    A.decl("g_attn_bc", [128, D], F32, 0, 1)
    A.decl("ln_g_bc", [128, 512], F32, 0, 1)
    A.decl("ln_b_bc", [128, 512], F32, 0, 1)
    A.decl("goa_bc", [128, 512], F32, 0, 1)
    A.decl("gq8_bc", [128, 64], F32, 0, 1)
    A.decl("gk_bc", [128, 64], F32, 0, 1)
    A.decl("WsT", [128, 8, 128], BF16, 0, 1)
    A.decl("WsTs", [64, 8, 64], BF16, 0, 1)
    A.decl("bs32", [8, 128], F32, 0, 1)
    A.decl("bss32", [8, 64], F32, 0, 1)
    A.decl("bstmp", [8, 128], F32, 0, 1)
    A.decl("bsl", [40, 128], BF16, 0, 1)
    A.decl("bsls", [40, 64], BF16, 0, 1)
    A.decl("blockind", [40, 512], BF16, 0, 1)
    A.decl("Win", [128, 8, 2560], BF16, 0, 1)
    A.decl("wst0", [128, 2560], F32, 0, 0)
    A.decl("wst1", [128, 2560], F32, 0, 0)
    A.decl("wst2", [128, 2560], F32, 0, 0)
    A.decl("fst0", [128, 8, 512], F32, 3, 4)
    A.decl("fst1", [128, 8, 512], F32, 3, 4)
    A.decl("hTs", [128, 8, 64], BF16, 0, 1)
    for nm in ("xt0", "xt1"):
        A.decl(nm, [128, D], F32, 1, 1)
    A.decl("h_bf", [128, D], BF16, 1, 1)
    A.decl("h_bf_1", [128, D], BF16, 1, 1)
    A.decl("u_1", [128, 512], F32, 1, 1)
    A.decl("vn_bf_1", [128, 512], BF16, 1, 1)
    A.decl("q_bf_1", [128, 512], BF16, 1, 1)
    A.decl("k_bf_1", [128, 512], BF16, 1, 1)
    A.decl("a_bf_1", [128, 512], BF16, 1, 1)
    A.decl("zk_sb", [128, 512], F32, 1, 1)
    A.decl("u", [128, 512], F32, 1, 1)
    A.decl("gv", [128, 512], F32, 1, 1)
    A.decl("wtmp", [128, 512], F32, 1, 1)
    A.decl("vn32", [128, 512], F32, 1, 1)
    A.decl("vn_bf", [128, 512], BF16, 1, 1)
    A.decl("sq", [128, D], F32, 1, 1)
    A.decl("k32_0", [128, 512], F32, 1, 1)
    A.decl("k32_1", [128, 512], F32, 1, 1)
    A.decl("v32_0", [128, 512], F32, 1, 1)
    A.decl("v32_1", [128, 512], F32, 1, 1)
    A.decl("tq", [128, 512], F32, 1, 1)
    A.decl("q_bf", [128, 512], BF16, 1, 1)
    A.decl("k_bf", [128, 512], BF16, 1, 1)
    A.decl("a32", [128, 512], F32, 1, 1)
    A.decl("a_bf", [128, 512], BF16, 1, 1)
    A.decl("hT", [128, 8, S], BF16, 1, 2)
    A.decl("qT", [128, 4, S], BF16, 1, 2)
    A.decl("kT", [128, 4, S], BF16, 1, 2)
    A.decl("Wv", [128, 8, 512], BF16, 2, 2)
    A.decl("mask2", [128, 256], BF16, 0, 2)
    A.decl("mixTa", [128, 4, S], BF16, 1, 3)
    A.decl("mixTb", [128, 4, S], BF16, 2, 3)
    A.decl("accA", [128, S], F32, 2, 2)
    A.decl("accB", [128, S], F32, 2, 2)
    A.decl("bT", [128, 4, S], F32, 2, 2)
    A.decl("ssqb", [128, S], F32, 2, 2)
    A.decl("tmpd", [128, S], F32, 2, 2)
    A.decl("sqb", [128, S], BF16, 2, 2)
    for i in range(3):
        A.decl("Vaug%d" % i, [128, 192], BF16, 2, 2)
    for i in range(4):
        A.decl("PT%d" % i, [128, 256], BF16, 2, 2)
    A.decl("Wo", [128, 8, D], BF16, 2, 4)
    A.decl("Wg", [128, 8, DFF], BF16, 3, 7)
    A.decl("Wu", [128, 8, DFF], BF16, 3, 7)
    A.decl("Wd", [128, NF, D], BF16, 5, 7)
    for i in range(4):
        A.decl("xa%d" % i, [128, D], F32, 3, 4)
    A.decl("xio0", [128, D], F32, 5, 7)
    A.decl("xio1", [128, D], F32, 5, 7)
    A.decl("h2_bf", [128, D], BF16, 5, 7)
    A.decl("Wo7", [128, 8, D], BF16, 6, 7)
    A.decl("junk3", [128, D], BF16, 3, 4)
    A.decl("h2Ts", [128, 8, 64], BF16, 7, 7)
    A.decl("actTs", [128, NF, 64], BF16, 7, 7)
    A.decl("h2T", [128, 8, 512], BF16, 5, 5)
    A.decl("actT", [128, NF, 512], BF16, 5, 5)
    A.decl("sil0", [128, 512], BF16, 5, 7)
    A.decl("sil1", [128, 512], BF16, 5, 7)
    for i in range(2):
        A.decl("Kt%d" % i, [128, 3, 512], BF16, 5, 5)
        A.decl("Vt%d" % i, [128, 3, 512], BF16, 5, 5)
        A.decl("qb%d" % i, [128, 512], BF16, 5, 5)
        A.decl("wV%d" % i, [128, 3, 512], BF16, 5, 5)
    for i in range(4):
        A.decl("pbf%d" % i, [128, 24], BF16, 5, 5)
    A.decl("s24_1", [128, 24], F32, 5, 5)
    A.decl("prod", [128, 512], F32, 5, 6)
    A.decl("s24", [128, 24], F32, 5, 5)
    A.decl("ksh", [64, 4, 512], F32, 6, 6)
    A.decl("vsh", [64, 4, 512], F32, 6, 6)
    A.decl("qs6", [64, 512], BF16, 6, 6)
    A.decl("trow_i", [1, 64], I32, 6, 6)
    A.decl("trow_f", [1, 64], F32, 6, 6)
    A.decl("sprod", [64, 512], F32, 6, 6)
    A.decl("snum", [64, 512], F32, 6, 6)
    A.decl("sb_bf", [64, 512], BF16, 6, 6)
    T = A.build()

    bank = [nc.alloc_psum_tensor("bank%d" % i, [128, 512], F32) for i in range(8)]

    def bankbf(i):
        return bank[i][:].bitcast(BF16)

    st = T["st"]

    def MM(out, lhsT, rhs, start, stop, reads, writes, signal=None):
        if signal is None:
            signal = bool(stop)
        return P.op("tensor", lambda e: e.matmul(out, lhsT=lhsT, rhs=rhs, start=start, stop=stop,
                                                 skip_group_check=True), reads=reads, writes=writes, signal=signal)

    def TR(out, in_, ident, reads, writes, signal=True):
        return P.op("tensor", lambda e: e.transpose(out=out, in_=in_, identity=ident), reads=reads, writes=writes,
                    signal=signal)

    def ACT(out, in_, func, reads, writes, **kw):
        return P.op("scalar", lambda e: e.activation(out=out, in_=in_, func=func, **kw), reads=reads, writes=writes)

    def ACOPY(out, in_, reads, writes):
        return P.op("scalar", lambda e: e.copy(out=out, in_=in_), reads=reads, writes=writes)

    def V(fn, reads, writes):
        return P.op("vector", fn, reads=reads, writes=writes)

    def G(fn, reads, writes):
        return P.op("gpsimd", fn, reads=reads, writes=writes)

    def rstd_chain(ap, scale, key):
        V(lambda e: e.tensor_scalar(out=ap, in0=ap, scalar1=scale, scalar2=EPS, op0=ALU.mult, op1=ALU.add), [key], [key])
        ACT(ap, ap, AF.Sqrt, [key], [key])
        V(lambda e: e.reciprocal(out=ap, in_=ap), [key], [key])

    ident, ones = T["ident"], T["ones"]
    G(lambda e: e.memset(ident[:], 1.0), [], ["ident"])
    G(lambda e: e.affine_select(out=ident[:], in_=ident[:], pattern=[[-1, 128]], compare_op=ALU.is_equal,
                                fill=0.0, base=0, channel_multiplier=1), ["ident"], ["ident"])
    G(lambda e: e.memset(ones[:], 1.0), [], ["ones"])
    V(lambda e: e.memset(st[:], 0.0), [], ["st"])
    V(lambda e: e.memset(T["st2"][:], 0.0), [], ["st2"])

    def bc_load(name, src, parts=128):
        P.dma("scalar", T[name][:], src.partition_broadcast(parts), [], [name])

    bc_load("g_attn_bc", g_attn)
    bc_load("g_ffn_bc", g_ffn)
    bc_load("ln_g_bc", ln_v_g)
    bc_load("ln_b_bc", ln_v_b)
    bc_load("goa_bc", g_out_a)
    bc_load("gob_bc", g_out_b, 64)
    bc_load("gq8_bc", g_q)
    bc_load("gk_bc", g_k)
    V(lambda e: e.tensor_scalar(out=T["gq8_bc"][:], in0=T["gq8_bc"][:], scalar1=0.125, scalar2=None, op0=ALU.mult),
      ["gq8_bc"], ["gq8_bc"])
    P.dma("sync", T["gob_col"][:], g_out_b.rearrange("(c p) -> p c", p=128), [], ["gob_col"],
          allow_slow_non_contiguous=True)

    WsT, WsTs = T["WsT"], T["WsTs"]
    P.dma("gpsimd", WsT[:], w_sT.rearrange("h j i -> j h i"), [], ["WsT"])
    G(lambda e: e.affine_select(out=WsT[:], in_=WsT[:], pattern=[[0, 8], [1, 128]], compare_op=ALU.is_ge,
                                fill=0.0, base=0, channel_multiplier=-1), ["WsT"], ["WsT"])
    wk = [("WsTs", b) for b in range(16)]
    G(lambda e: e.memset(WsTs[:], 0.0), [], wk)
    for b in range(16):
        P.dma("gpsimd", WsTs[4 * b:4 * b + 4, :, 4 * b:4 * b + 4],
              w_sT[:, 0:4, 0:4].rearrange("h j i -> j h i"), [], [("WsTs", b)], allow_slow_non_contiguous=True)
    G(lambda e: e.affine_select(out=WsTs[:], in_=WsTs[:], pattern=[[0, 8], [1, 64]], compare_op=ALU.is_ge,
                                fill=0.0, base=0, channel_multiplier=-1), wk, wk + ["WsTs"])
    bs32, bss32, bstmp, bsl, bsls, blockind = T["bs32"], T["bss32"], T["bstmp"], T["bsl"], T["bsls"], T["blockind"]
    P.dma("sync", bs32[:], b_s, [], ["bs32"])
    P.dma("sync", bss32[:].rearrange("h (s t) -> h s t", t=4), b_s[:, 0:4].unsqueeze(1).to_broadcast([8, 16, 4]),
          [], ["bss32"], allow_slow_non_contiguous=True)
    for (src, dst, w, sk, dk) in ((bs32, bsl, 128, "bs32", "bsl"), (bss32, bsls, 64, "bss32", "bsls")):
        V(lambda e, dst=dst: e.memset(dst[:], 0.0), [], [dk])
        V(lambda e, src=src, dst=dst, w=w: e.tensor_copy(out=dst[0:8, 0:w], in_=src[0:8, 0:w]), [sk], [dk])
        V(lambda e, src=src, dst=dst, w=w: e.tensor_tensor(out=bstmp[0:8, 0:w], in0=src[0:8, 0:w], in1=dst[0:8, 0:w],
                                                            op=ALU.subtract), [sk, dk], ["bstmp"])
        V(lambda e, dst=dst, w=w: e.tensor_copy(out=dst[32:40, 0:w], in_=bstmp[0:8, 0:w]), ["bstmp"], [dk])
    G(lambda e: e.memset(blockind[:], 0.0), [], ["blockind"])
    G(lambda e: e.memset(blockind[0:8, :], 1.0), ["blockind"], ["blockind"])
    G(lambda e: e.affine_select(out=blockind[0:8, :], in_=blockind[0:8, :], pattern=[[1, 512]], compare_op=ALU.is_ge,
                                fill=0.0, base=0, channel_multiplier=-64), ["blockind"], ["blockind"])
    G(lambda e: e.affine_select(out=blockind[0:8, :], in_=blockind[0:8, :], pattern=[[-1, 512]], compare_op=ALU.is_ge,
                                fill=0.0, base=63, channel_multiplier=64), ["blockind"], ["blockind"])
    ACOPY(blockind[32:40, :], blockind[0:8, :], ["blockind"], ["blockind"])
    mask2 = T["mask2"]
    G(lambda e: e.memset(mask2[:], 1.0), [], ["mask2"])
    G(lambda e: e.affine_select(out=mask2[:, 0:128], in_=mask2[:, 0:128], pattern=[[-1, 128]], compare_op=ALU.is_ge,
                                fill=0.0, base=0, channel_multiplier=1), ["mask2"], ["mask2"])
    G(lambda e: e.affine_select(out=mask2[:, 128:256], in_=mask2[:, 128:256], pattern=[[1, 128]], compare_op=ALU.is_ge,
                                fill=0.0, base=0, channel_multiplier=-1), ["mask2"], ["mask2"])
    selw, mb = T["selw"], T["mb"]
    G(lambda e: e.memset(selw[:], 0.0), [], ["selw"])
    G(lambda e: e.memset(selw[:, 63:64], 1.0), ["selw"], ["selw"])
    G(lambda e: e.memset(mb[:], 0.0), [], ["mb"])
    for t in range(1, 4):
        G(lambda e, t=t: e.affine_select(out=mb[:, t * 24:t * 24 + 8], in_=mb[:, t * 24:t * 24 + 8], pattern=[[0, 8]],
                                         compare_op=ALU.is_ge, fill=NEG, base=-t, channel_multiplier=1), ["mb"], ["mb"])

    Win, Wv, Wo, Wg, Wu, Wd = T["Win"], T["Wv"], T["Wo"], T["Wg"], T["Wu"], T["Wd"]
    w_in_v = w_in.rearrange("(k p) n -> p k n", p=128)
    wst = [T["wst0"], T["wst1"], T["wst2"]]
    for k in range(8):
        sb_ = wst[k % 3]
        P.dma("sync", sb_[:], w_in_v[:, k, :], [], ["wst%d" % (k % 3)])
        ACOPY(Win[:, k, 0:896], sb_[:, 0:896], ["wst%d" % (k % 3)], [("Win", k)])
        V(lambda e, k=k, sb_=sb_: e.tensor_copy(out=Win[:, k, 896:1920], in_=sb_[:, 896:1920]), ["wst%d" % (k % 3)], [("Winh", k)])
        G(lambda e, k=k, sb_=sb_: e.tensor_copy(out=Win[:, k, 1920:2560], in_=sb_[:, 1920:2560]), ["wst%d" % (k % 3)], [("Wing", k)])

    hT, hTs, qT, kT, mixTa, mixTb, mixTs = T["hT"], T["hTs"], T["qT"], T["kT"], T["mixTa"], T["mixTb"], T["mixTs"]
    xts = [T["xt0"], T["xt1"]]
    k32s = [T["k32_0"], T["k32_1"]]
    v32s = [T["v32_0"], T["v32_1"]]
    h_bf, u_sb, gv, wtmp, vn32, vn_bf, sq = T["h_bf"], T["u"], T["gv"], T["wtmp"], T["vn32"], T["vn_bf"], T["sq"]
    tq, q_bf, k_bf, a32, a_bf = T["tq"], T["q_bf"], T["k_bf"], T["a32"], T["a_bf"]
    ks32, vs32, qs_bf = T["ks32"], T["vs32"], T["qs_bf"]

    seq1 = [(True, 0)] + [(False, ti) for ti in range(NT)]
    NS1 = len(seq1)
    u2 = [T["u"], T["u_1"]]
    hb2 = [T["h_bf"], T["h_bf_1"]]
    vnb2 = [T["vn_bf"], T["vn_bf_1"]]
    qb2 = [T["q_bf"], T["q_bf_1"]]
    kb2 = [T["k_bf"], T["k_bf_1"]]
    ab2 = [T["a_bf"], T["a_bf_1"]]
    zq_sb, zk_sb = T["tq"], T["zk_sb"]

    def s1info(n):
        sample, ti = seq1[n]
        M = 64 if sample else 128
        par = n % 2
        cols = slice(0, 64) if sample else slice(ti * 128, (ti + 1) * 128)
        sb = n * 48
        return sample, ti, M, par, cols, sb, ("st", n)

    def s1_ld(n):
        sample, ti, M, par, cols, sb, stk = s1info(n)
        src = xs if sample else xp[ti * 128:(ti + 1) * 128, :]
        P.dma("sync", xts[par][0:M, :], src, [], ["xt%d" % par])

    def s1_fa(n):
        sample, ti, M, par, cols, sb, stk = s1info(n)
        xt, xk = xts[par], "xt%d" % par
        src = xs if sample else xp[ti * 128:(ti + 1) * 128, :]
        hb, hbk = hb2[par], "h_bf%d" % par
        sc0 = st[0:M, sb:sb + 1]
        ACT(sq[0:M, :], xt[0:M, :], AF.Square, [xk], ["sq", stk], accum_out=sc0)
        rstd_chain(sc0, 1.0 / D, stk)
        V(lambda e: e.scalar_tensor_tensor(out=hb[0:M, :], in0=xt[0:M, :], scalar=sc0, in1=T["g_attn_bc"][0:M, :],
                                           op0=ALU.mult, op1=ALU.mult), [xk, stk, "g_attn_bc"], [hbk])

    def s1_fb(n, part):
        sample, ti, M, par, cols, sb, stk = s1info(n)
        hb, hbk = hb2[par], "h_bf%d" % par
        hTd = hTs if sample else hT
        hk = "hTs" if sample else ("hT", ti)
        u_sb, uk = u2[par], "u%d" % par
        vnb, vnk = vnb2[par], "vn_bf%d" % par
        qbf, qk_ = qb2[par], "q_bf%d" % par
        kbf, kk_ = kb2[par], "k_bf%d" % par

        def sc(j, w=1):
            return st[0:M, sb + j:sb + j + w]

        if part == "tr":
            trb = bankbf(5)
            for k in range(8):
                TR(trb[:, k * 128:k * 128 + M], hb[0:M, k * 128:(k + 1) * 128], ident[0:M, 0:M], [hbk, "ident"], ["b5"],
                   signal=(k == 7))
            ACOPY(hTd[:, :, cols], trb.rearrange("p (k m) -> p k m", m=128)[:, :, 0:M], ["b5"], [hk])
            return
        if part == "proj":
            s1_proj(n)
            return
        s1_chain(n)

    def s1_proj(n):
        sample, ti, M, par, cols, sb, stk = s1info(n)
        hTd = hTs if sample else hT
        hk = "hTs" if sample else ("hT", ti)
        u_sb, uk = u2[par], "u%d" % par

        def sc(j, w=1):
            return st[0:M, sb + j:sb + j + w]

        for k in range(8):
            for c in range(5):
                MM(bank[c][0:M, :], hTd[:, k, cols], Win[:, k, c * 512:(c + 1) * 512], k == 0, k == 7,
                   [hk], ["b%d" % c])
        k32 = ks32 if sample else k32s[par]
        v32 = vs32 if sample else v32s[par]
        k32k = "ks32" if sample else "k32_%d" % par
        v32k = "vs32" if sample else "v32_%d" % par
        ACT(u_sb[0:M, :], bank[0][0:M, :], AF.Gelu_apprx_tanh, ["b0"], [uk])
        ACT(gv[0:M, :], bank[1][0:M, :], AF.Gelu_apprx_tanh, ["b1"], ["gv", stk], accum_out=sc(2))
        V(lambda e: e.tensor_copy(out=zq_sb[0:M, :], in_=bank[2][0:M, :]), ["b2"], ["zq_sb"])
        V(lambda e: e.tensor_copy(out=zk_sb[0:M, :], in_=bank[3][0:M, :]), ["b3"], ["zk_sb"])
        ACOPY(v32[0:M, :], bank[4][0:M, :], ["b4"], [v32k])

    def s1_chain(n):
        sample, ti, M, par, cols, sb, stk = s1info(n)
        vnb, vnk = vnb2[par], "vn_bf%d" % par
        qbf, qk_ = qb2[par], "q_bf%d" % par
        kbf, kk_ = kb2[par], "k_bf%d" % par
        k32 = ks32 if sample else k32s[par]
        v32 = vs32 if sample else v32s[par]
        k32k = "ks32" if sample else "k32_%d" % par
        v32k = "vs32" if sample else "v32_%d" % par

        def sc(j, w=1):
            return st[0:M, sb + j:sb + j + w]

        V(lambda e: e.tensor_scalar(out=sc(3), in0=sc(2), scalar1=-1.0 / 512, scalar2=None, op0=ALU.mult), [stk], [stk])
        ACT(sq[0:M, 0:512], gv[0:M, :], AF.Square, ["gv", stk], ["sq", stk], bias=sc(3), scale=1.0, accum_out=sc(7))
        ACT(sq[0:M, 512:1024], zq_sb[0:M, :], AF.Square, ["zq_sb"], ["sqq"])
        V(lambda e: e.reduce_sum(out=sc(8, 8), in_=sq[0:M, 512:1024].rearrange("p (h d) -> p h d", d=64), axis=AX.X),
          ["sqq"], [stk])
        ACT(sq[0:M, 512:1024], zk_sb[0:M, :], AF.Square, ["zk_sb"], ["sqq"])
        V(lambda e: e.reduce_sum(out=sc(16, 8), in_=sq[0:M, 512:1024].rearrange("p (h d) -> p h d", d=64), axis=AX.X),
          ["sqq"], [stk])
        V(lambda e: e.tensor_scalar(out=sc(7), in0=sc(7), scalar1=1.0 / 512, scalar2=EPS, op0=ALU.mult, op1=ALU.add), [stk], [stk])
        V(lambda e: e.tensor_scalar(out=sc(8, 16), in0=sc(8, 16), scalar1=1.0 / 64, scalar2=EPS, op0=ALU.mult, op1=ALU.add),
          [stk], [stk])
        ACT(sc(7, 17), sc(7, 17), AF.Sqrt, [stk], [stk])
        V(lambda e: e.reciprocal(out=sc(7, 17), in_=sc(7, 17)), [stk], [stk])
        V(lambda e: e.scalar_tensor_tensor(out=wtmp[0:M, :], in0=gv[0:M, :], scalar=sc(3), in1=T["ln_g_bc"][0:M, :],
                                           op0=ALU.add, op1=ALU.mult), ["gv", stk, "ln_g_bc"], ["wtmp"])
        if sample:
            V(lambda e: e.scalar_tensor_tensor(out=vn32[0:M, :], in0=wtmp[0:M, :], scalar=sc(7), in1=T["ln_b_bc"][0:M, :],
                                               op0=ALU.mult, op1=ALU.add), ["wtmp", stk, "ln_b_bc"], ["vn32"])
            V(lambda e: e.tensor_copy(out=vnb[0:M, :], in_=vn32[0:M, :]), ["vn32"], [vnk])
            P.dma("sync", nvc_s, vn32[0:M, :], ["vn32"], [])
        else:
            V(lambda e: e.scalar_tensor_tensor(out=vnb[0:M, :], in0=wtmp[0:M, :], scalar=sc(7), in1=T["ln_b_bc"][0:M, :],
                                               op0=ALU.mult, op1=ALU.add), ["wtmp", stk, "ln_b_bc"], [vnk])
        V(lambda e: e.tensor_tensor(out=zq_sb[0:M, :].rearrange("p (h d) -> p h d", d=64),
                                    in0=zq_sb[0:M, :].rearrange("p (h d) -> p h d", d=64),
                                    in1=sc(8, 8).unsqueeze(2).to_broadcast([M, 8, 64]), op=ALU.mult), ["zq_sb", stk], ["zq_sb"])
        V(lambda e: e.tensor_tensor(out=qbf[0:M, :].rearrange("p (h d) -> p h d", d=64),
                                    in0=zq_sb[0:M, :].rearrange("p (h d) -> p h d", d=64),
                                    in1=T["gq8_bc"][0:M, :].unsqueeze(1).to_broadcast([M, 8, 64]), op=ALU.mult),
          ["zq_sb", "gq8_bc"], [qk_])
        V(lambda e: e.tensor_tensor(out=zk_sb[0:M, :].rearrange("p (h d) -> p h d", d=64),
                                    in0=zk_sb[0:M, :].rearrange("p (h d) -> p h d", d=64),
                                    in1=sc(16, 8).unsqueeze(2).to_broadcast([M, 8, 64]), op=ALU.mult), ["zk_sb", stk], ["zk_sb"])
        V(lambda e: e.tensor_tensor(out=k32[0:M, :].rearrange("p (h d) -> p h d", d=64),
                                    in0=zk_sb[0:M, :].rearrange("p (h d) -> p h d", d=64),
                                    in1=T["gk_bc"][0:M, :].unsqueeze(1).to_broadcast([M, 8, 64]), op=ALU.mult),
          ["zk_sb", "gk_bc"], [k32k])
        if sample:
            P.dma("sync", nk_s, k32[0:M, :], [k32k], [])
            P.dma("sync", nv_s, v32[0:M, :], [v32k], [])
            V(lambda e: e.tensor_copy(out=qs_bf[:], in_=qbf[0:64, :]), [qk_], ["qs_bf"])
            P.dma("sync", qs_scr, qs_bf[:], ["qs_bf"], ["qs_scr"])
        else:
            P.dma("sync", nk_p[ti * 128:(ti + 1) * 128, :], k32[:], [k32k], [])
            P.dma("sync", nv_p[ti * 128:(ti + 1) * 128, :], v32[:], [v32k], [])
            G(lambda e: e.tensor_copy(out=kbf[:], in_=k32[:]), [k32k], [kk_])

    def s1_bk1(n):
        sample, ti, M, par, cols, sb, stk = s1info(n)
        u_sb, uk = u2[par], "u%d" % par
        vnb, vnk = vnb2[par], "vn_bf%d" % par
        qbf, qk_ = qb2[par], "q_bf%d" % par
        kbf, kk_ = kb2[par], "k_bf%d" % par
        abf, ak_ = ab2[par], "a_bf%d" % par
        sc6 = st[0:M, sb + 6:sb + 7]
        Wm, bl = (WsTs, bsls) if sample else (WsT, bsl)
        Wk = "WsTs" if sample else "WsT"
        for h in range(8):
            MM(bank[6][0:M, h * 64:(h + 1) * 64], Wm[0:M, h, 0:M], vnb[0:M, h * 64:(h + 1) * 64], h == 0, False,
               [Wk, vnk], ["b6"])
        MM(bank[6][0:M, :], bl[0:40, 0:M], blockind[0:40, :], False, True, ["bsls" if sample else "bsl", "blockind"], ["b6"])
        if DEBUG and not sample and ti == 0:
            pass

    def s1_bk1a(n):
        sample, ti, M, par, cols, sb, stk = s1info(n)
        u_sb, uk = u2[par], "u%d" % par
        abf, ak_ = ab2[par], "a_bf%d" % par
        sc6 = st[0:M, sb + 6:sb + 7]
        V(lambda e: e.tensor_tensor(out=a32[0:M, :], in0=u_sb[0:M, :], in1=bank[6][0:M, :], op=ALU.mult), [uk, "b6"], ["a32"])
        ACT(sq[0:M, 0:512], a32[0:M, :], AF.Square, ["a32"], ["sq", stk], accum_out=sc6)
        rstd_chain(sc6, 1.0 / 512, stk)
        V(lambda e: e.scalar_tensor_tensor(out=abf[0:M, :], in0=a32[0:M, :], scalar=sc6, in1=T["goa_bc"][0:M, :],
                                           op0=ALU.mult, op1=ALU.mult), ["a32", stk, "goa_bc"], [ak_])

    def s1_bk1b(n):
        sample, ti, M, par, cols, sb, stk = s1info(n)
        qbf, qk_ = qb2[par], "q_bf%d" % par
        kbf, kk_ = kb2[par], "k_bf%d" % par
        if not sample:
            t7 = bankbf(7)
            for j in range(4):
                TR(t7[:, j * 128:(j + 1) * 128], qbf[:, j * 128:(j + 1) * 128], ident[:], [qk_, "ident"], ["b7"], signal=False)
            for j in range(4):
                TR(t7[:, (4 + j) * 128:(5 + j) * 128], kbf[:, j * 128:(j + 1) * 128], ident[:], [kk_, "ident"], ["b7"], signal=(j == 3))
            t7v = t7.rearrange("p (k m) -> p k m", m=128)
            ACOPY(qT[:, :, cols], t7v[:, 0:4, :], ["b7"], [("qT", ti)])
            ACOPY(kT[:, :, cols], t7v[:, 4:8, :], ["b7"], [("kT", ti)])

    def s1_bk2(n):
        sample, ti, M, par, cols, sb, stk = s1info(n)
        abf, ak_ = ab2[par], "a_bf%d" % par
        t7 = bankbf(7)
        for j in range(4):
            TR(t7[:, j * 128:j * 128 + M], abf[0:M, j * 128:(j + 1) * 128], ident[0:M, 0:M], [ak_, "ident"], ["b7"], signal=(j == 3))
        mT = mixTs if sample else mixTa
        V(lambda e: e.tensor_copy(out=mT[:, 0:4, cols], in_=t7.rearrange("p (k m) -> p k m", m=128)[:, 0:4, 0:M]), ["b7"],
          ["mixTs_a" if sample else ("mixT_a", ti)])

    P.barrier()
    s1_ld(0)
    s1_ld(1)
    s1_fa(0)
    s1_fb(0, "tr")
    s1_fa(1)
    for i in range(NS1 + 2):
        if i + 2 < NS1:
            s1_ld(i + 2)
        if i < NS1:
            s1_fb(i, "proj")
        if i + 1 < NS1:
            s1_fb(i + 1, "tr")
        if 0 <= i - 2 < NS1:
            s1_bk2(i - 2)
        if 0 <= i - 1 < NS1:
            s1_bk1(i - 1)
            s1_bk1b(i - 1)
            s1_bk1a(i - 1)
        if i < NS1:
            s1_fb(i, "chain")
        if i + 2 < NS1:
            s1_fa(i + 2)
    P.barrier()

    accs = [T["accA"], T["accB"]]
    bT, ssqb, tmpd, sqb = T["bT"], T["ssqb"], T["tmpd"], T["sqb"]
    Vaug = [T["Vaug0"], T["Vaug1"], T["Vaug2"]]
    PT = [T["PT0"], T["PT1"], T["PT2"], T["PT3"]]
    P.dma("gpsimd", Wv[:], w_in_v[:, :, 2048:2560], [], ["Wv"])
    for i in range(3):
        G(lambda e, i=i: e.memset(Vaug[i][:], 1.0), [], ["Vaug%d" % i])
    w_o_v = w_o.rearrange("(k p) n -> p k n", p=128)
    for k2 in range(2):
        P.dma("gpsimd", Wo[:, 4 * k2:4 * k2 + 4, :], w_o_v[:, 4 * k2:4 * k2 + 4, :], [], [("Wo", k2)])

    def pos(dil, r, n):
        b0 = dil * 128 * n + r
        return slice(b0, b0 + 127 * dil + 1, dil)

    SB = [(1, 2), (5, 6)]
    OB = (3, 4)

    def item_front(w, it):
        pair, dil, r, n = it
        slot = w % 3
        kp = pos(dil, r, n)
        for k in range(8):
            MM(bank[0][:, 0:128], hT[:, k, kp], Wv[:, k, pair * 128:(pair + 1) * 128], k == 0, k == 7, ["Wv"], ["b0"])
        ACOPY(Vaug[slot][:].rearrange("p (a d) -> p a d", d=64)[:, 0:3:2, :],
              bank[0][:, 0:128].rearrange("p (a d) -> p a d", d=64), ["b0"], ["Vaug%d" % slot])
        if n > 0:
            kpp = pos(dil, r, n - 1)
            for hd in range(2):
                hb = 64 * hd
                bk = SB[w % 2][hd]
                MM(bank[bk][:, 0:128], kT[hb:hb + 64, pair, kpp], qT[hb:hb + 64, pair, kp], True, True, [], ["b%d" % bk],
                   signal=False)
        for hd in range(2):
            hb = 64 * hd
            bk = SB[w % 2][hd]
            MM(bank[bk][:, 128:256], kT[hb:hb + 64, pair, kp], qT[hb:hb + 64, pair, kp], n == 0, True, [], ["b%d" % bk])
        for hd in range(2):
            bk = SB[w % 2][hd]
            pt = PT[(w % 2) * 2 + hd]
            ptk = "PT%d" % ((w % 2) * 2 + hd)
            lo = 0 if n > 0 else 128
            ACT(pt[:, lo:256], bank[bk][:, lo:256], AF.Exp, ["b%d" % bk], [ptk])
            V(lambda e, pt=pt, lo=lo: e.tensor_tensor(out=pt[:, lo:256], in0=pt[:, lo:256], in1=mask2[:, lo:256], op=ALU.mult),
              [ptk, "mask2"], [ptk])

    def item_back(w, it):
        pair, dil, r, n = it
        slot = w % 3
        pslot = (w - 1) % 3
        qp = pos(dil, r, n)
        for hd in range(2):
            pt = PT[(w % 2) * 2 + hd]
            ptk = "PT%d" % ((w % 2) * 2 + hd)
            ob = OB[hd]
            hs = slice(0, 128) if hd == 0 else slice(64, 192)
            if n > 0:
                MM(bank[ob][:, 0:128], Vaug[pslot][:, hs], pt[:, 0:128], True, False, ["Vaug%d" % pslot, ptk], ["b%d" % ob])
            MM(bank[ob][:, 0:128], Vaug[slot][:, hs], pt[:, 128:256], n == 0, True, ["Vaug%d" % slot, ptk], ["b%d" % ob])
            acc = accs[hd]
            ak = "acc%d" % hd
            if dil == 1:
                ACOPY(acc[:, qp], bank[ob][:, 0:128], ["b%d" % ob], [ak])
            else:
                V(lambda e, acc=acc, ob=ob, qp=qp: e.tensor_tensor(out=acc[:, qp], in0=acc[:, qp], in1=bank[ob][:, 0:128],
                                                                   op=ALU.add), ["b%d" % ob, ak], [ak])

    for pair in range(4):
        items = []
        for dil in (1, 4, 16):
            for r in range(dil):
                for n in range(S // dil // 128):
                    items.append((pair, dil, r, n))
        for w, it in enumerate(items):
            item_front(w, it)
            if w > 0:
                item_back(w - 1, items[w - 1])
        item_back(len(items) - 1, items[-1])
        ACOPY(tmpd[0:64, :], accs[0][64:128, :], ["acc0"], ["tmpd"])
        ACOPY(tmpd[64:128, :], accs[1][0:64, :], ["acc1"], ["tmpd"])
        V(lambda e: e.reciprocal(out=tmpd[:], in_=tmpd[:]), ["tmpd"], ["tmpd"])
        V(lambda e, pair=pair: e.tensor_tensor(out=bT[0:64, pair, :], in0=accs[0][0:64, :], in1=tmpd[0:64, :], op=ALU.mult),
          ["acc0", "tmpd"], [("bT", pair)])
        V(lambda e, pair=pair: e.tensor_tensor(out=bT[64:128, pair, :], in0=accs[1][64:128, :], in1=tmpd[64:128, :], op=ALU.mult),
          ["acc1", "tmpd"], [("bT", pair)])
        ACT(sqb[:], bT[:, pair, :], AF.Square, [("bT", pair)], ["sqb"])
        for c in range(4):
            MM(bank[7][:, :], ones[:], sqb[:, c * 512:(c + 1) * 512], True, True, ["ones", "sqb"], ["b7"])
            if pair == 0:
                V(lambda e, c=c: e.tensor_copy(out=ssqb[:, c * 512:(c + 1) * 512], in_=bank[7][:, :]), ["b7"], ["ssqb"])
            else:
                V(lambda e, c=c: e.tensor_tensor(out=ssqb[:, c * 512:(c + 1) * 512], in0=ssqb[:, c * 512:(c + 1) * 512],
                                                 in1=bank[7][:, :], op=ALU.add), ["b7", "ssqb"], ["ssqb"])
    rstd_chain(ssqb[:], 1.0 / 512, "ssqb")
    for pair in range(4):
        V(lambda e, pair=pair: e.scalar_tensor_tensor(out=mixTb[:, pair, :], in0=bT[:, pair, :],
                                                      scalar=T["gob_col"][:, pair:pair + 1], in1=ssqb[:],
                                                      op0=ALU.mult, op1=ALU.mult),
          [("bT", pair), "ssqb", "gob_col"], [("mixT_b", pair)])
    P.barrier()
    if DEBUG:
        P.dma("sync", d_mixTa, mixTa[:], [], [])
        P.dma("sync", d_mixTb, mixTb[:], [], [])
        P.barrier()

    wg_v = w_gate.rearrange("(k p) n -> p k n", p=128)
    wu_v = w_up.rearrange("(k p) n -> p k n", p=128)
    wd_v = w_down.rearrange("(f p) n -> p f n", p=128)
    wdma = []
    fst = [T["fst0"], T["fst1"]]
    jj = 0
    pend = []
    for blk in range(6):
        c0, c1 = blk * 512, min(DFF, blk * 512 + 512)
        for (Wt, wv_, nm) in ((Wg, wg_v, "Wg"), (Wu, wu_v, "Wu")):
            sb_ = fst[jj % 2]
            wdma.append(P.dma("scalar", sb_[:, :, 0:c1 - c0], wv_[:, :, c0:c1], [], ["fst%d" % (jj % 2)]))
            pend.append((Wt, sb_, c0, c1, nm, blk, jj % 2))
            if len(pend) == 2:
                Wt_, sb2, a0, a1, nm_, blk_, par_ = pend.pop(0)
                ACOPY(Wt_[:, :, a0:a1], sb2[:, :, 0:a1 - a0], ["fst%d" % par_], [(nm_, blk_)])
            jj += 1
    while pend:
        Wt_, sb2, a0, a1, nm_, blk_, par_ = pend.pop(0)
        ACOPY(Wt_[:, :, a0:a1], sb2[:, :, 0:a1 - a0], ["fst%d" % par_], [(nm_, blk_)])

    xas = [T["xa%d" % i] for i in range(4)]

    def wo_load(ti):
        P.dma("sync", xas[ti % 4][:], xp[ti * 128:(ti + 1) * 128, :], [], ["xa%d" % (ti % 4)])
    OBK = [(0, 1), (2, 3)]

    def wo_tile(ti, sample):
        M = 64 if sample else 128
        par = ti % 2
        xa = T["xio0"] if sample else xas[ti % 4]
        xk = "xio0" if sample else "xa%d" % (ti % 4)
        WoT = T["Wo7"] if sample else Wo
        src = xs if sample else xp[ti * 128:(ti + 1) * 128, :]
        dst = y_s if sample else y_p[ti * 128:(ti + 1) * 128, :]
        cols = slice(0, 64) if sample else slice(ti * 128, (ti + 1) * 128)
        if sample:
            P.dma("sync", xa[0:M, :], src, [], [xk])
        for k in range(8):
            mTk = mixTs[:, k, cols] if sample else (mixTa[:, k, cols] if k < 4 else mixTb[:, k - 4, cols])
            for c in range(2):
                bk = OBK[par][c]
                MM(bank[bk][0:M, :], mTk, WoT[:, k, c * 512:(c + 1) * 512], k == 0, k == 7,
                   [("Wo7" if sample else "Wo", k // 4)] + (["mixTs_a", "mixTs_b"] if sample else []), ["b%d" % bk])
        for c in range(2):
            bk = OBK[par][c]
            V(lambda e, c=c, bk=bk: e.tensor_tensor(out=xa[0:M, c * 512:(c + 1) * 512], in0=xa[0:M, c * 512:(c + 1) * 512],
                                                    in1=bank[bk][0:M, :], op=ALU.add), [xk, "b%d" % bk], [xk])
        P.dma("sync" if sample else "gpsimd", dst, xa[0:M, :], [xk], [("x1", 16 if sample else ti)])
        if not sample:
            col = 8 + ti
            ACT(T["junk3"][0:M, :], xa[0:M, :], AF.Square, [xk], ["junk3", "st2"], accum_out=T["st2"][0:M, col:col + 1])
            rstd_chain(T["st2"][0:M, col:col + 1], 1.0 / D, "st2")

    for ti in range(3):
        wo_load(ti)
    for ti in range(NT):
        if ti + 3 < NT:
            wo_load(ti + 3)
        wo_tile(ti, False)
    P.barrier(skip=wdma)
    for blk in range(6):
        f0, f1 = blk * 4, min(NF, blk * 4 + 4)
        P.dma("gpsimd", Wd[:, f0:f1, :], wd_v[:, f0:f1, :], [], [("Wd", blk)])

    Kt = [T["Kt0"], T["Kt1"]]
    Vt = [T["Vt0"], T["Vt1"]]
    qb = [T["qb0"], T["qb1"]]
    wV = [T["wV0"], T["wV1"]]
    pbf = [T["pbf0"], T["pbf1"], T["pbf2"], T["pbf3"]]
    prod, s24 = T["prod"], T["s24"]
    samp_state = {"n": 0}

    def samp_rows(t, b):
        if b == 0:
            return slice(1920, 2048, 1)
        if b == 1:
            return slice(1536 + t, 2048, 4)
        return slice(t, 2048, 16)

    def samp_pf_k(idx):
        if idx >= 64:
            return
        sq_, t = idx // 4, idx % 4
        sl = idx % 2
        for b in range(3):
            P.dma("gpsimd", Kt[sl][:, b, :], ck[sq_, samp_rows(t, b), :], [], [("Kt", sl, b)])
        P.dma("sync", qb[sl][:], qs_scr[idx, :].partition_broadcast(128), ["qs_scr"], ["qb%d" % sl])

    def samp_pf_v(idx):
        if idx >= 64:
            return
        sq_, t = idx // 4, idx % 4
        sl = idx % 2
        for b in range(3):
            P.dma("gpsimd", Vt[sl][:, b, :], cv[sq_, samp_rows(t, b), :], [], [("Vt", sl, b)])

    s24s = [T["s24"], T["s24_1"]]

    def samp_scores(idx):
        sq_, t = idx // 4, idx % 4
        sl = idx % 2
        s24_ = s24s[idx % 2]
        sk = "s24_%d" % (idx % 2)
        for b in range(3):
            V(lambda e, b=b: e.tensor_tensor(out=prod[:], in0=Kt[sl][:, b, :], in1=qb[sl][:], op=ALU.mult),
              [("Kt", sl, b), "qb%d" % sl], ["prod"])
            V(lambda e, b=b: e.reduce_sum(out=s24_[:, b * 8:(b + 1) * 8], in_=prod[:].rearrange("p (h d) -> p h d", d=64),
                                          axis=AX.X), ["prod"], [sk])
        if t > 0:
            V(lambda e: e.tensor_tensor(out=s24_[:], in0=s24_[:], in1=mb[:, t * 24:(t + 1) * 24], op=ALU.add), [sk, "mb"], [sk])

    def samp_exp(idx):
        p4 = idx % 4
        ACT(pbf[p4][:], s24s[idx % 2][:], AF.Exp, ["s24_%d" % (idx % 2)], ["pbf%d" % p4])

    def samp_wv(idx):
        sl = idx % 2
        p4 = idx % 4
        G(lambda e: e.tensor_tensor(out=wV[sl][:, 0:2, :].rearrange("p b (h d) -> p (b h) d", d=64),
                                    in0=Vt[sl][:, 0:2, :].rearrange("p b (h d) -> p (b h) d", d=64),
                                    in1=pbf[p4][:, 0:16].unsqueeze(2).to_broadcast([128, 16, 64]), op=ALU.mult),
          [("Vt", sl, 0), ("Vt", sl, 1), "pbf%d" % p4], [("wV", sl, 0)])
        V(lambda e: e.tensor_tensor(out=wV[sl][:, 2, :].rearrange("p (h d) -> p h d", d=64),
                                    in0=Vt[sl][:, 2, :].rearrange("p (h d) -> p h d", d=64),
                                    in1=pbf[p4][:, 16:24].unsqueeze(2).to_broadcast([128, 8, 64]), op=ALU.mult),
          [("Vt", sl, 2), "pbf%d" % p4], [("wV", sl, 1)])

    def samp_back(idx):
        sl = idx % 2
        p4 = idx % 4
        sel = selw[:, 63 - idx:127 - idx]
        for b in range(3):
            first = (idx == 0 and b == 0)
            last = (idx == 63 and b == 2)
            MM(bank[6][0:64, :], sel, wV[sl][:, b, :], first, last, ["selw", ("wV", sl, 0), ("wV", sl, 1)], ["b6"])
            MM(bank[7][0:64, 0:8], sel, pbf[p4][:, b * 8:(b + 1) * 8], first, last, ["selw", "pbf%d" % p4], ["b7"],
               signal=(b == 2))

    NSTEP = 67

    def samp_step():
        i = samp_state["n"]
        if i >= NSTEP:
            return
        if i == 0:
            samp_pf_k(0)
            samp_pf_k(1)
        if 1 <= i <= 64:
            samp_exp(i - 1)
        if 2 <= i <= 65:
            samp_wv(i - 2)
        samp_pf_v(i)
        if i < 64:
            samp_scores(i)
        samp_pf_k(i + 2)
        if 3 <= i <= 66:
            samp_back(i - 3)
        samp_state["n"] = i + 1

    xios = [T["xio0"], T["xio1"]]
    h2_bf, h2T, actT = T["h2_bf"], T["h2T"], T["actT"]
    sils = [T["sil0"], T["sil1"]]

    def ffn_prep_a(g, j, sample):
        M = 64 if sample else 128
        ti = 16 if sample else g * 4 + j
        xio = xios[j % 2]
        xk = "xio%d" % (j % 2)
        src = y_s if sample else y_p[ti * 128:(ti + 1) * 128, :]
        col = 8 + ti
        P.dma("sync", xio[0:M, :], src, [("x1", ti)], [xk])
        if sample:
            ACT(h2_bf[0:M, :], xio[0:M, :], AF.Square, [xk], ["h2_bf", "st2"], accum_out=T["st2"][0:M, col:col + 1])
            rstd_chain(T["st2"][0:M, col:col + 1], 1.0 / D, "st2")
        V(lambda e: e.scalar_tensor_tensor(out=h2_bf[0:M, :], in0=xio[0:M, :], scalar=T["st2"][0:M, col:col + 1],
                                           in1=T["g_ffn_bc"][0:M, :], op0=ALU.mult, op1=ALU.mult),
          [xk, "st2", "g_ffn_bc"], ["h2_bf"])

    def ffn_prep_b(g, j, sample):
        M = 64 if sample else 128
        h2T_ = T["h2Ts"] if sample else h2T
        t0b = bankbf(0)
        for k in range(8):
            TR(t0b[:, k * 128:k * 128 + M], h2_bf[0:M, k * 128:(k + 1) * 128], ident[0:M, 0:M], ["h2_bf", "ident"], ["b0"], signal=(k == 7))
        ACOPY(h2T_[:, :, j * 128:j * 128 + M], t0b.rearrange("p (k m) -> p k m", m=128)[:, :, 0:M], ["b0"], ["h2T"])

    def ffn_gateup(g, sample):
        NTOK = 64 if sample else 512
        h2T_ = T["h2Ts"] if sample else h2T
        actT_ = T["actTs"] if sample else actT
        for f in range(NF):
            gb, ub = (0, 2) if f % 2 == 0 else (1, 3)
            for k in range(8):
                MM(bank[gb][:, 0:NTOK], Wg[:, k, f * 128:(f + 1) * 128], h2T_[:, k, 0:NTOK], k == 0, k == 7,
                   ["h2T", ("Wg", f // 4)], ["b%d" % gb])
            for k in range(8):
                MM(bank[ub][:, 0:NTOK], Wu[:, k, f * 128:(f + 1) * 128], h2T_[:, k, 0:NTOK], k == 0, k == 7,
                   ["h2T", ("Wu", f // 4)], ["b%d" % ub])
            sl = sils[f % 2]
            ACT(sl[:, 0:NTOK], bank[gb][:, 0:NTOK], AF.Silu, ["b%d" % gb], ["sil%d" % (f % 2)])
            V(lambda e, f=f, sl=sl, ub=ub: e.tensor_tensor(out=actT_[:, f, 0:NTOK], in0=sl[:, 0:NTOK], in1=bank[ub][:, 0:NTOK],
                                                           op=ALU.mult), ["sil%d" % (f % 2), "b%d" % ub], [("actT", f)])
            if not sample and f % 2 == 1:
                samp_step()

    def ffn_down_mm(g, j, sample):
        M = 64 if sample else 128
        actT_ = T["actTs"] if sample else actT
        bks = (4, 5) if j % 2 == 0 else (2, 3)
        for c in range(2):
            bk = bks[c]
            for f in range(NF):
                MM(bank[bk][0:M, :], actT_[:, f, j * 128:j * 128 + M], Wd[:, f, c * 512:(c + 1) * 512], f == 0, f == NF - 1,
                   [("actT", f), ("Wd", f // 4)], ["b%d" % bk])

    def ffn_down_fin(g, j, sample):
        M = 64 if sample else 128
        ti = 16 if sample else g * 4 + j
        xio = xios[j % 2]
        xk = "xio%d" % (j % 2)
        src = y_s if sample else y_p[ti * 128:(ti + 1) * 128, :]
        bks = (4, 5) if j % 2 == 0 else (2, 3)
        P.dma("sync", xio[0:M, :], src, [("x1", ti)], [xk])
        for c in range(2):
            bk = bks[c]
            V(lambda e, c=c, bk=bk: e.tensor_tensor(out=xio[0:M, c * 512:(c + 1) * 512], in0=xio[0:M, c * 512:(c + 1) * 512],
                                                    in1=bank[bk][0:M, :], op=ALU.add), [xk, "b%d" % bk], [xk])
        P.dma("sync", src, xio[0:M, :], [xk], [("yout", ti)])

    for j in range(4):
        ffn_prep_a(0, j, False)
        ffn_prep_b(0, j, False)
        samp_step()
    for g in range(4):
        ffn_gateup(g, False)
        if g < 3:
            ffn_prep_a(g + 1, 0, False)
        for j in range(4):
            ffn_down_mm(g, j, False)
            if g < 3:
                ffn_prep_b(g + 1, j, False)
                if j < 3:
                    ffn_prep_a(g + 1, j + 1, False)
            ffn_down_fin(g, j, False)
            samp_step()
    while samp_state["n"] < NSTEP:
        samp_step()

    P.barrier()
    for k2 in range(2):
        P.dma("gpsimd", T["Wo7"][:, 4 * k2:4 * k2 + 4, :], w_o_v[:, 4 * k2:4 * k2 + 4, :], [], [("Wo7", k2)])
    ksh, vsh, sprod, snum, sb_bf, qs6 = T["ksh"], T["vsh"], T["sprod"], T["snum"], T["sb_bf"], T["qs6"]
    st2 = T["st2"]
    V(lambda e: e.memset(ksh[:], 0.0), [], [("ksh", j) for j in range(4)])
    V(lambda e: e.memset(vsh[:], 0.0), [], [("vsh", j) for j in range(4)])
    P.dma("sync", qs6[:], qs_scr, [], ["qs6"])
    for j in range(0, 4):
        P.dma("sync", ksh[j:64, j, :], nk_s[0:64 - j, :], [], [("ksh", j)])
        P.dma("sync", vsh[j:64, j, :], nv_s[0:64 - j, :], [], [("vsh", j)])
    tcol = st2[0:64, 30:31]
    bself = st2[0:64, 32:64]
    trow_i, trow_f = T["trow_i"], T["trow_f"]
    G(lambda e: e.iota(out=trow_i[:].rearrange("o (s t) -> o s t", t=4), pattern=[[0, 16], [1, 4]], base=0,
                       channel_multiplier=0), [], ["trow_i"])
    V(lambda e: e.tensor_copy(out=trow_f[:], in_=trow_i[:]), ["trow_i"], ["trow_f"])
    P.dma("sync", tscr.rearrange("(o n) -> o n", o=1), trow_f[:], ["trow_f"], ["tscr"])
    P.dma("sync", tcol, tscr.rearrange("(p o) -> p o", o=1), ["tscr", "st2"], ["st2"])
    V(lambda e: e.memset(bself[:, 0:8], float(np.log(3.0))), ["st2"], ["st2"])
    for j in range(1, 4):
        V(lambda e, j=j: e.tensor_scalar(out=bself[:, j * 8:(j + 1) * 8], in0=tcol.to_broadcast([64, 8]), scalar1=float(j),
                                         scalar2=None, op0=ALU.is_ge), ["st2"], ["st2"])
        V(lambda e, j=j: e.tensor_scalar(out=bself[:, j * 8:(j + 1) * 8], in0=bself[:, j * 8:(j + 1) * 8], scalar1=-1.0,
                                         scalar2=-NEG, op0=ALU.add, op1=ALU.mult), ["st2"], ["st2"])
    ss = prod[0:64, 0:32]
    ee = prod[0:64, 64:96]
    for j in range(4):
        kin = ksh[:, j, :]
        V(lambda e, kin=kin: e.tensor_tensor(out=sprod[:], in0=qs6[:], in1=kin, op=ALU.mult), ["qs6", ("ksh", j)], ["sprod"])
        V(lambda e, j=j: e.reduce_sum(out=ss[:, j * 8:(j + 1) * 8], in_=sprod[:].rearrange("p (h d) -> p h d", d=64), axis=AX.X),
          ["sprod"], ["prod"])
    V(lambda e: e.tensor_tensor(out=ss, in0=ss, in1=bself, op=ALU.add), ["prod", "st2"], ["prod"])
    ACT(ee, ss, AF.Exp, ["prod"], ["prod"])
    V(lambda e: e.tensor_copy(out=snum[:], in_=bank[6][0:64, :]), ["b6"], ["snum"])
    for j in range(4):
        vin = vsh[:, j, :]
        V(lambda e, j=j, vin=vin: e.tensor_tensor(out=sprod[:].rearrange("p (h d) -> p h d", d=64),
                                                  in0=vin.rearrange("p (h d) -> p h d", d=64),
                                                  in1=ee[:, j * 8:(j + 1) * 8].unsqueeze(2).to_broadcast([64, 8, 64]),
                                                  op=ALU.mult), [("vsh", j), "prod"], ["sprod"])
        V(lambda e: e.tensor_tensor(out=snum[:], in0=snum[:], in1=sprod[:], op=ALU.add), ["snum", "sprod"], ["snum"])
    den = st2[0:64, 16:24]
    V(lambda e: e.tensor_copy(out=den, in_=bank[7][0:64, 0:8]), ["b7"], ["st2"])
    for j in range(4):
        V(lambda e, j=j: e.tensor_tensor(out=den, in0=den, in1=ee[:, j * 8:(j + 1) * 8], op=ALU.add), ["st2", "prod"], ["st2"])
    V(lambda e: e.reciprocal(out=den, in_=den), ["st2"], ["st2"])
    V(lambda e: e.tensor_tensor(out=snum[:].rearrange("p (h d) -> p h d", d=64), in0=snum[:].rearrange("p (h d) -> p h d", d=64),
                                in1=den.unsqueeze(2).to_broadcast([64, 8, 64]), op=ALU.mult), ["snum", "st2"], ["snum"])
    ACT(sprod[:], snum[:], AF.Square, ["snum"], ["sprod", "st2"], accum_out=st2[0:64, 26:27])
    rstd_chain(st2[0:64, 26:27], 1.0 / 512, "st2")
    V(lambda e: e.scalar_tensor_tensor(out=sb_bf[:], in0=snum[:], scalar=st2[0:64, 26:27], in1=T["gob_bc"][:],
                                       op0=ALU.mult, op1=ALU.mult), ["snum", "st2", "gob_bc"], ["sb_bf"])
    t5 = bankbf(5)
    for j in range(4):
        TR(t5[:, j * 128:j * 128 + 64], sb_bf[:, j * 128:(j + 1) * 128], ident[0:64, 0:64], ["sb_bf", "ident"], ["b5"])
    ACOPY(mixTs[:, 4:8, :], t5.rearrange("p (k m) -> p k m", m=128)[:, 0:4, 0:64], ["b5"], ["mixTs_b"])
    P.barrier()
    if DEBUG:
        P.dma("sync", d_mixTs, mixTs[:], [], [])
    wo_tile(16, True)
    ffn_prep_a(4, 0, True)
    ffn_prep_b(4, 0, True)
    ffn_gateup(4, True)
    ffn_down_mm(4, 0, True)
    ffn_down_fin(4, 0, True)

    P.finish()
    P.emit()
    return nc, P


_CACHE = {}


def kernel(x_prompt, x_sample, cache_k, cache_v, g_attn, w_in, ln_v_g, ln_v_b, w_s, b_s, g_q, g_k,
           g_out_a, g_out_b, w_o, g_ffn, w_gate, w_up, w_down):
    f = lambda a: np.ascontiguousarray(np.asarray(a, dtype=np.float32))
    if "nc" not in _CACHE:
        _CACHE["nc"] = build_program()[0]
    nc = _CACHE["nc"]
    shared = {
        "g_attn": f(g_attn[0]), "w_in": f(w_in[0]), "ln_v_g": f(ln_v_g[0]), "ln_v_b": f(ln_v_b[0]),
        "w_sT": f(np.transpose(np.asarray(w_s[0]), (0, 2, 1))), "b_s": f(b_s[0]), "g_q": f(g_q[0]), "g_k": f(g_k[0]),
        "g_out_a": f(g_out_a[0]), "g_out_b": f(g_out_b[0]), "w_o": f(w_o[0]), "g_ffn": f(g_ffn[0]),
        "w_gate": f(w_gate[0]), "w_up": f(w_up[0]), "w_down": f(w_down[0]),
    }
    xpn, xsn = np.asarray(x_prompt), np.asarray(x_sample)
    ckn, cvn = np.asarray(cache_k), np.asarray(cache_v)
    in_maps = []
    for c in range(8):
        m = dict(shared)
        m["xp"] = f(xpn[c])
        m["xs"] = f(xsn[16 * c:16 * c + 16].reshape(64, D))
        m["ck"] = f(ckn[0, 16 * c:16 * c + 16].reshape(16, 2048, 512))
        m["cv"] = f(cvn[0, 16 * c:16 * c + 16].reshape(16, 2048, 512))
        in_maps.append(m)
    res = run_bass_kernel_spmd(nc, in_maps, core_ids=list(range(8))).results
    cat = lambda k: np.stack([np.asarray(r[k], dtype=np.float32) for r in res], 0)
    y_prompt = cat("y_p")
    y_sample = cat("y_s").reshape(128, 4, D)
    nkp = cat("nk_p").reshape(1, 8, S, 8, 64)
    nvp = cat("nv_p").reshape(1, 8, S, 8, 64)
    nks = cat("nk_s").reshape(1, 128, 4, 8, 64)
    nvs = cat("nv_s").reshape(1, 128, 4, 8, 64)
    nvc = cat("nvc_s").reshape(1, 128, 4, 512)
    return (y_prompt, y_sample, nkp, nvp, nks, nvs, nvc)
```
